# Optimizing a Trainium2 kernel written in Bass

```python
import math
import jax, jax.numpy as jnp
from jax import lax
import numpy as np

D_MODEL = 1024
BATCH = 8
SEQ = 4096
DEPTH = 1

MEM_LEN = 256
EPS = 1e-6
GLA_HEADS = 4
GLA_DK = 64
GLA_DV = 128
GLA_RANK = 16
GLA_TAU = 16.0
GLA_CHUNK = 64
SWA_HEADS = 8
SWA_KV_HEADS = 2
SWA_DH = 64
WINDOW = 128
REL_BUCKETS = 32
REL_MAX_DIST = 128
CROSS_HEADS = 4
CROSS_DH = D_MODEL // CROSS_HEADS
D_FF = 2816
CONV_WIDTH = 3

GLA_KW = GLA_HEADS * GLA_DK
GLA_VW = GLA_HEADS * GLA_DV
SWA_QW = SWA_HEADS * SWA_DH
SWA_KVW = SWA_KV_HEADS * SWA_DH
D_MIX = GLA_VW + SWA_QW
IN_SPLITS = (GLA_KW, GLA_KW, GLA_VW, GLA_VW, GLA_RANK, SWA_QW, SWA_KVW, SWA_KVW)
D_IN = 2320

kernel_name = "hybrid_gla_swa_parallel_heads"


def rms_norm(x, g):
    xf = x.astype(jnp.float32)
    y = xf * lax.rsqrt(jnp.mean(xf * xf, axis=-1, keepdims=True) + EPS)
    return (y * g.astype(jnp.float32)).astype(x.dtype)


def t5_bucket(dist):
    n = jnp.maximum(dist, 0)
    max_exact = REL_BUCKETS // 2
    nf = jnp.maximum(n, 1).astype(jnp.float32)
    large = max_exact + (jnp.log(nf / max_exact) / math.log(REL_MAX_DIST / max_exact)
                         * (REL_BUCKETS - max_exact)).astype(jnp.int32)
    large = jnp.minimum(large, REL_BUCKETS - 1)
    return jnp.where(n < max_exact, n, large)


def gla_mixer(q, k, v, r, a_low, w_alpha, b_alpha, gla_gain):
    f32 = jnp.float32
    B, T, _ = q.shape
    C = GLA_CHUNK
    N = T // C
    log_a = jax.nn.log_sigmoid((a_low @ w_alpha + b_alpha).astype(f32)) / GLA_TAU

    def heads(t, d):
        return t.astype(f32).reshape(B, N, C, GLA_HEADS, d).transpose(0, 3, 1, 2, 4)

    qh = heads(q, GLA_DK) * (GLA_DK ** -0.5)
    kh = heads(k, GLA_DK)
    vh = heads(v, GLA_DV)
    bcum = jnp.cumsum(heads(log_a, GLA_DK), axis=3)
    b_last = bcum[:, :, :, -1:, :]
    q_dec = qh * jnp.exp(bcum)
    k_inv = kh * jnp.exp(-bcum)
    k_to_end = kh * jnp.exp(b_last - bcum)
    causal = jnp.tril(jnp.ones((C, C), dtype=bool))
    att = jnp.where(causal, jnp.einsum('bhnik,bhnjk->bhnij', q_dec, k_inv), 0.0)
    o_intra = jnp.einsum('bhnij,bhnjv->bhniv', att, vh)
    kv = jnp.einsum('bhnck,bhncv->nbhkv', k_to_end, vh)
    decay = jnp.exp(b_last[:, :, :, 0, :]).transpose(2, 0, 1, 3)

    def step(S, inp):
        d, kv_n = inp
        return d[..., None] * S + kv_n, S

    S0 = jnp.zeros((B, GLA_HEADS, GLA_DK, GLA_DV), f32)
    _, S_prev = lax.scan(step, S0, (decay, kv))
    o_inter = jnp.einsum('bhnck,nbhkv->bhncv', q_dec, S_prev)
    o = (o_intra + o_inter).transpose(0, 2, 3, 1, 4).reshape(B, T, GLA_HEADS, GLA_DV)
    o = o * lax.rsqrt(jnp.mean(o * o, axis=-1, keepdims=True) + EPS) * gla_gain.astype(f32)
    o = o.reshape(B, T, GLA_VW) * jax.nn.silu(r.astype(f32))
    return o.astype(q.dtype)


def swa_mixer(q, k, v, rel_bias, sinks):
    f32 = jnp.float32
    B, T, _ = q.shape
    L = WINDOW
    NB = T // L
    G = SWA_HEADS // SWA_KV_HEADS
    qb = q.astype(f32).reshape(B, NB, L, SWA_KV_HEADS, G, SWA_DH) * (SWA_DH ** -0.5)
    kb = k.astype(f32).reshape(B, NB, L, SWA_KV_HEADS, SWA_DH)
    vb = v.astype(f32).reshape(B, NB, L, SWA_KV_HEADS, SWA_DH)

    def with_prev(t):
        prev = jnp.pad(t, ((0, 0), (1, 0), (0, 0), (0, 0), (0, 0)))[:, :-1]
        return jnp.concatenate([prev, t], axis=2)

    kk = with_prev(kb)
    vv = with_prev(vb)
    s = jnp.einsum('bnqhgd,bnkhd->bnhgqk', qb, kk)
    qpos = jnp.arange(L)[:, None] + L
    kpos = jnp.arange(2 * L)[None, :]
    dist = qpos - kpos
    bias = rel_bias.astype(f32)[t5_bucket(dist)]
    bias = bias.transpose(2, 0, 1).reshape(SWA_KV_HEADS, G, L, 2 * L)
    in_window = (dist >= 0) & (dist < WINDOW)
    has_prev = (jnp.arange(NB)[:, None, None] > 0) | (kpos[None] >= L)
    mask = in_window[None] & has_prev
    s = jnp.where(mask[None, :, None, None], s + bias, -jnp.inf)
    sink = sinks.astype(f32).reshape(SWA_KV_HEADS, G)[None, None, :, :, None, None]
    m = jnp.maximum(jnp.max(s, axis=-1, keepdims=True), sink)
    p = jnp.exp(s - m)
    p = p / (jnp.sum(p, axis=-1, keepdims=True) + jnp.exp(sink - m))
    o = jnp.einsum('bnhgqk,bnkhd->bnqhgd', p, vv)
    return o.reshape(B, T, SWA_QW).astype(q.dtype)


def cross_attn(h, memn, w_q, w_k, w_v, w_o):
    B, T, _ = h.shape
    q = (h @ w_q).reshape(B, T, CROSS_HEADS, CROSS_DH)
    k = (memn @ w_k).reshape(B, MEM_LEN, CROSS_HEADS, CROSS_DH)
    v = (memn @ w_v).reshape(B, MEM_LEN, CROSS_HEADS, CROSS_DH)
    s = jnp.einsum('bqhd,bkhd->bhqk', q, k).astype(jnp.float32) * (CROSS_DH ** -0.5)
    p = jax.nn.softmax(s, axis=-1).astype(v.dtype)
    o = jnp.einsum('bhqk,bkhd->bqhd', p, v).reshape(B, T, D_MODEL)
    return o @ w_o


def conv_ffn(h, w_up, conv_w, conv_b, w_down):
    u = h @ w_up
    u = lax.conv_general_dilated(u, conv_w[:, None, :].astype(u.dtype), window_strides=(1,),
                                 padding=[(CONV_WIDTH - 1, 0)],
                                 dimension_numbers=('NWC', 'WIO', 'NWC'),
                                 feature_group_count=2 * D_FF) + conv_b
    gate, val = jnp.split(u, 2, axis=-1)
    return (jax.nn.silu(gate) * val) @ w_down


def setup_inputs(seed: int = 0) -> dict:
    key = jax.random.key(seed)
    ks = jax.random.split(key, 24)
    f32 = jnp.float32

    def nrm(k, shape, scale):
        return jax.random.normal(k, shape, f32) * scale

    def gain(k, shape):
        return 1.0 + 0.02 * jax.random.normal(k, shape, f32)

    return {
        "x": nrm(ks[0], (BATCH, SEQ, D_MODEL), 1.0),
        "mem": nrm(ks[1], (BATCH, MEM_LEN, D_MODEL), 1.0),
        "norm_mix": gain(ks[2], (DEPTH, D_MODEL)),
        "w_in": nrm(ks[3], (DEPTH, D_MODEL, D_IN), D_MODEL ** -0.5),
        "w_alpha": nrm(ks[4], (DEPTH, GLA_RANK, GLA_KW), GLA_RANK ** -0.5),
        "b_alpha": nrm(ks[5], (DEPTH, GLA_KW), 0.1),
        "gla_gain": gain(ks[6], (DEPTH, GLA_DV)),
        "rel_bias": nrm(ks[7], (REL_BUCKETS, SWA_HEADS), 0.5),
        "sinks": nrm(ks[8], (DEPTH, SWA_HEADS), 0.5),
        "w_out": nrm(ks[9], (DEPTH, D_MIX, D_MODEL), D_MIX ** -0.5),
        "norm_cross": gain(ks[10], (DEPTH, D_MODEL)),
        "norm_mem": gain(ks[11], (DEPTH, D_MODEL)),
        "w_q_c": nrm(ks[12], (DEPTH, D_MODEL, D_MODEL), D_MODEL ** -0.5),
        "w_k_c": nrm(ks[13], (DEPTH, D_MODEL, D_MODEL), D_MODEL ** -0.5),
        "w_v_c": nrm(ks[14], (DEPTH, D_MODEL, D_MODEL), D_MODEL ** -0.5),
        "w_o_c": nrm(ks[15], (DEPTH, D_MODEL, D_MODEL), D_MODEL ** -0.5),
        "norm_ffn": gain(ks[16], (DEPTH, D_MODEL)),
        "w_up": nrm(ks[17], (DEPTH, D_MODEL, 2 * D_FF), D_MODEL ** -0.5),
        "conv_w": nrm(ks[18], (DEPTH, CONV_WIDTH, 2 * D_FF), CONV_WIDTH ** -0.5),
        "conv_b": nrm(ks[19], (DEPTH, 2 * D_FF), 0.01),
        "w_down": nrm(ks[20], (DEPTH, D_FF, D_MODEL), D_FF ** -0.5),
        "norm_final": gain(ks[21], (D_MODEL,)),
    }


def reference(x, mem, norm_mix, w_in, w_alpha, b_alpha, gla_gain, rel_bias, sinks, w_out,
              norm_cross, norm_mem, w_q_c, w_k_c, w_v_c, w_o_c, norm_ffn, w_up, conv_w, conv_b,
              w_down, norm_final):
    split_at = [int(s) for s in np.cumsum(IN_SPLITS)[:-1]]
    h = x
    for l in range(DEPTH):
        hn = rms_norm(h, norm_mix[l])
        proj = hn @ w_in[l]
        gq, gk, gv, gr, ga, sq, sk, sv = jnp.split(proj, split_at, axis=-1)
        o_gla = gla_mixer(gq, gk, gv, gr, ga, w_alpha[l], b_alpha[l], gla_gain[l])
        o_swa = swa_mixer(sq, sk, sv, rel_bias, sinks[l])
        h = h + jnp.concatenate([o_gla, o_swa], axis=-1) @ w_out[l]
        h = h + cross_attn(rms_norm(h, norm_cross[l]), rms_norm(mem, norm_mem[l]),
                           w_q_c[l], w_k_c[l], w_v_c[l], w_o_c[l])
        h = h + conv_ffn(rms_norm(h, norm_ffn[l]), w_up[l], conv_w[l], conv_b[l], w_down[l])
    return rms_norm(h, norm_final)
```

```python
import math
from contextlib import ExitStack

import numpy as np
import concourse.bass as bass
import concourse.mybir as mybir
from concourse.bass_utils import run_bass_kernel_spmd

F32 = mybir.dt.float32
BF16 = mybir.dt.bfloat16
AF = mybir.ActivationFunctionType
ALU = mybir.AluOpType
AX = mybir.AxisListType

COMPUTE = ("pe", "act", "dve", "pool")
N_DMA_SEMS = 6


class Prog:
    def __init__(self, nc):
        self.nc = nc
        self.streams = {e: [] for e in ("pe", "act", "dve", "pool", "sp")}
        self.last_w = {}
        self.readers = {}
        self.dma_cnt = {"sp": 0, "pool": 0}
        self.bank_rr = 0
        self.arena_toks = []
        self.fence_deps = []
        self.fenced = set()

    @staticmethod
    def _is_arena(k):
        return isinstance(k[0], str) and k[0].startswith("@")

    @staticmethod
    def _compress(toks):
        best = {}
        for t in toks:
            if t[0] == "c":
                key, v = ("c", t[1]), t[2]
            else:
                key, v = ("d", t[1], t[2]), t[3]
            if v > best.get(key, -1):
                best[key] = v
        return [(k + (v,)) for k, v in best.items()]

    def fence(self):
        self.fence_deps = self._compress(self.arena_toks + self.fence_deps)
        self.arena_toks = []
        self.fenced = set()

    def _deps(self, reads, writes):
        deps = []
        for k in list(reads) + list(writes):
            if self._is_arena(k) and k not in self.fenced:
                deps.extend(self.fence_deps)
                self.fenced.add(k)
        for r in reads:
            if r in self.last_w:
                deps.append(self.last_w[r])
        for w in writes:
            if w in self.last_w:
                deps.append(self.last_w[w])
            deps.extend(self.readers.get(w, ()))
        return deps

    def _commit(self, tok, reads, writes):
        if any(self._is_arena(k) for k in list(reads) + list(writes)):
            self.arena_toks.append(tok)
            if len(self.arena_toks) > 8192:
                self.arena_toks = self._compress(self.arena_toks)
        for r in reads:
            self.readers.setdefault(r, []).append(tok)
        for w in writes:
            self.last_w[w] = tok
            self.readers[w] = []

    def op(self, eng, fn, reads=(), writes=()):
        st = self.streams[eng]
        idx = len(st)
        tok = ("c", eng, idx)
        deps = self._deps(reads, writes)
        st.append(dict(kind="op", fn=fn, deps=deps, tok=tok))
        self._commit(tok, reads, writes)
        return tok

    def dma(self, q, out, in_, reads=(), writes=(), **kw):
        st = self.streams[q]
        n = self.dma_cnt[q]
        self.dma_cnt[q] = n + 1
        semi = n % N_DMA_SEMS
        val = 16 * (n // N_DMA_SEMS + 1)
        tok = ("d", q, semi, val)
        deps = self._deps(reads, writes)
        if n >= N_DMA_SEMS:
            deps.append(("d", q, semi, val - 16))
        st.append(dict(kind="dma", out=out, in_=in_, deps=deps, tok=tok, kw=kw))
        self._commit(tok, reads, writes)
        return tok

    def bank(self):
        b = self.bank_rr
        self.bank_rr = (b + 1) % 8
        return b

    def emit(self, sems, dsems, engines):
        needed = {e: set() for e in COMPUTE}
        for e, st in self.streams.items():
            for o in st:
                for d in o["deps"]:
                    if d[0] == "c":
                        if d[1] == e and e in ("pe", "sp"):
                            continue
                        needed[d[1]].add(d[2])
        sigval = {}
        for e in COMPUTE:
            for rank, idx in enumerate(sorted(needed[e])):
                sigval[(e, idx)] = rank + 1

        def run(e, eng):
            waited = {}
            st = self.streams[e]
            for o in st:
                want = {}
                for d in o["deps"]:
                    if d[0] == "c":
                        if d[1] == e and e in ("pe", "sp"):
                            continue
                        key = ("c", d[1])
                        v = sigval[(d[1], d[2])]
                    else:
                        key = ("d", d[1], d[2])
                        v = d[3]
                    if v > waited.get(key, 0):
                        want[key] = max(want.get(key, 0), v)
                for key, v in want.items():
                    sem = sems[key[1]] if key[0] == "c" else dsems[key[1]][key[2]]
                    eng.wait_ge(sem, v)
                    waited[key] = v
                if o["kind"] == "op":
                    ins = o["fn"](eng)
                    t = o["tok"]
                    if (t[1], t[2]) in sigval:
                        ins.then_inc(sems[e], 1)
                else:
                    t = o["tok"]
                    eng.dma_start(out=o["out"], in_=o["in_"], **o["kw"]).then_inc(
                        dsems[t[1]][t[2]], 16)
            if e in ("sp", "pool"):
                n = self.dma_cnt[e]
                for semi in range(min(n, N_DMA_SEMS)):
                    cnt = (n - 1 - semi) // N_DMA_SEMS + 1
                    eng.wait_ge(dsems[e][semi], 16 * cnt)

        for e, eng_deco in engines.items():
            eng_deco(lambda eng, e=e: run(e, eng))


D = 1024
NQ_TOK = 1024
EPS = 1e-6
WAVES = [(0, 4), (4, 8), (8, 12), (12, 16), (16, 20), (20, 22)]
SUBS = [(0, 342), (342, 342), (684, 340)]
D_FF = 2816
D_IN = 2320
MIX_STOP = 99
ARENA_WORDS = 11976


class Arena:
    def __init__(self, t, words):
        self.t = t
        self.words = words
        self.off = 0

    def reset(self):
        self.off = 0

    def alloc(self, shape, dt):
        n = 1
        for s_ in shape[1:]:
            n *= s_
        esz = 4 if dt == F32 else 2
        words = (n * esz + 3) // 4
        words = (words + 1) // 2 * 2
        a = self.t[:, self.off:self.off + words]
        self.off += words
        assert self.off <= self.words, ("arena overflow", self.off, self.words)
        if dt != F32:
            a = a.bitcast(dt)
        a = a[:, 0:n]
        if len(shape) > 2:
            names = "abcd"[:len(shape) - 1]
            pat = "p (" + " ".join(names) + ") -> p " + " ".join(names)
            a = a.rearrange(pat, **{nm: int(sz) for nm, sz in zip(names, shape[1:])})
        return a


def build(T, phases=("mix", "cross", "ffn")):
    nc = bass.Bass("TRN2", target_bir_lowering=False)
    MIX, CROSS, FFN = ("mix" in phases), ("cross" in phases), ("ffn" in phases)

    def din(name, shape):
        return nc.dram_tensor(name, list(shape), F32, kind="ExternalInput").ap()

    x = din("x", [T, D])
    gvec_in = din("gvec", [5, 128, 8])
    ident_in = din("ident", [128, 128])
    y = nc.dram_tensor("y", [T, D], F32, kind="ExternalOutput").ap()
    if FFN:
        convp_in = din("convp", [128, 44, 4])
        w_up = din("w_up", [D, 2 * D_FF])
        w_down = din("w_down", [D_FF, D])
        w_up_v = w_up.rearrange("(kc p) (gv j n) -> p kc gv j n", p=128, gv=2, n=128)
        w_down_v = w_down.rearrange("(j p) n -> p j n", p=128)
    if CROSS:
        mem = din("mem", [256, D])
        wq_v = din("w_q_c", [D, D]).rearrange("(kc p) n -> p kc n", p=128)
        wk_v = din("w_k_c", [D, D]).rearrange("(kc p) n -> p kc n", p=128)
        wv_v = din("w_v_c", [D, D]).rearrange("(kc p) n -> p kc n", p=128)
        wo_v = din("w_o_c", [D, D]).rearrange("(kc p) n -> p kc n", p=128)
    if MIX:
        w_in_v = din("w_in", [D, D_IN]).rearrange("(kc p) n -> p kc n", p=128)
        w_out_v = din("w_out", [D, D]).rearrange("(kc p) n -> p kc n", p=128)
        walpha_in = din("walpha", [32, 256])
        gain_in = din("gain", [128, 1])
        maskA_in = din("maskA", [128, 128])
        tri_in = din("tri", [2, 128, 128])
        bias_in = din("swabias", [128, 2, 2, 512])
        negm_in = din("negmask", [128, 2, 512])
        sinks_in = din("sinks_b", [128, 8])

    NQ = T // NQ_TOK
    with ExitStack() as es:
        def sb(name, shape, dt):
            return es.enter_context(nc.sbuf_tensor(name, shape, dt))

        hT = sb("hT", [128, 8, NQ_TOK], F32)
        ident = sb("identf", [128, 128], F32)
        ones_bf = sb("ones_bf", [128, 128], BF16)
        ones1 = sb("ones1", [128, 128], BF16)
        gvec = sb("gvec_sb", [128, 5, 8], F32)
        G_FIN, G_FFN, G_CROSS, G_MEM, G_MIX = range(5)
        if FFN:
            convp = sb("convp_sb", [128, 44, 4], F32)
            wup = sb("wup", [128, 3, 8, 2, 128], BF16)
            wdn = sb("wdn", [128, 4, 1024], BF16)
            halo = sb("halo", [128, 8, 2], BF16)
        if CROSS:
            wq = sb("wq", [128, 8, 1024], BF16)
            wo = sb("wo", [128, 8, 1024], BF16)
            KT = sb("KT", [128, 8, 256], BF16)
            Vm = sb("Vm", [128, 2, 1024], BF16)
        if MIX:
            w_in = sb("w_in_sb", [128, 8, D_IN], BF16)
            skdup = sb("skdup", [128, 8, 256], BF16)
            w_out = sb("w_out_sb", [128, 8, 1024], BF16)
            identb = sb("identb", [128, 128], BF16)
            walpha = sb("walpha_sb", [128, 256], BF16)
            gain = sb("gain_sb", [128, 1], F32)
            maskA = sb("maskA_sb", [128, 128], BF16)
            triLE = sb("triLE", [128, 128], F32)
            triGT = sb("triGT", [128, 128], F32)
            biasm = sb("biasm", [128, 2, 2, 512], BF16)
            esink = sb("esink", [128, 8], F32)
            S = sb("S_state", [128, 2, 128], F32)
            Sbf = sb("S_bf", [128, 2, 128], BF16)
            skT = sb("skT", [128, 2, 640], BF16)
            vext = sb("vext", [128, 2, 2, 65], BF16)
        arena_t = sb("arena", [128, ARENA_WORDS], F32)
        A = Arena(arena_t, ARENA_WORDS)

        banks = [es.enter_context(nc.psum_tensor(f"bank{i}", [128, 512], F32)) for i in range(8)]
        banks_bf = [bk.bitcast(BF16) for bk in banks]
        sems = {e: es.enter_context(nc.semaphore(f"s_{e}")) for e in COMPUTE}
        dsems = {q: [es.enter_context(nc.semaphore(f"d_{q}{i}")) for i in range(N_DMA_SEMS)]
                 for q in ("sp", "pool")}

        P = Prog(nc)

        A.reset()
        xtok = A.alloc([128, 4, 1024], F32)
        A.reset()
        f_sq = A.alloc([128, 2, 512], BF16)
        f_rstd = A.alloc([128, 512], F32)
        ytok = A.alloc([128, 2, 1024], F32)
        yT = A.alloc([128, 8, 512], F32)
        if FFN:
            A.reset()
            c_sq = A.alloc([128, 2, 512], BF16)
            c_rstd = A.alloc([128, 512], F32)
            hn3 = A.alloc([128, 8, NQ_TOK + 2], BF16)
            actT = A.alloc([128, 2, 4, NQ_TOK], BF16)
            tA = A.alloc([128, 2, 2, 344], F32)
            tB = A.alloc([128, 2, 2, 344], F32)
        if CROSS:
            A.reset()
            b_sq = A.alloc([128, 2, 512], BF16)
            b_rstd = A.alloc([128, 512], F32)
            hn2 = A.alloc([128, 8, 512], BF16)
            cq = A.alloc([128, 8, 512], BF16)
            pT = A.alloc([128, 2, 2, 512], BF16)
            rden = A.alloc([128, 2, 512], F32)
            onT = A.alloc([128, 8, 512], BF16)
            A.reset()
            k_sq = A.alloc([128, 2, 512], BF16)
            k_rstd = A.alloc([128, 512], F32)
            k_xtok = A.alloc([128, 2, 1024], F32)
            memT = A.alloc([128, 8, 256], F32)
            memn = A.alloc([128, 8, 256], BF16)
            wk = A.alloc([128, 8, 1024], BF16)
            wv = wk
        if MIX:
            A.reset()
            bias_f = A.alloc([128, 2, 2, 512], F32)
            negm = A.alloc([128, 2, 512], F32)
            sinks_b = A.alloc([128, 8], F32)
            A.reset()
            er = A.alloc([128, 512], F32)
            m_sq = er.bitcast(BF16).rearrange("p (a b) -> p a b", a=2)
            m_rstd = A.alloc([128, 512], F32)
            alow = m_rstd.bitcast(BF16)[:, 0:512]
            junk = m_rstd[:, 256:384]
            hnT = A.alloc([128, 8, 512], BF16)
            L = A.alloc([128, 1024], F32)
            Eend = L
            Ebc = A.alloc([128, 2, 512], F32)
            Einv = A.alloc([128, 2, 512], F32)
            qdec = A.alloc([128, 2, 512], BF16)
            kinv = A.alloc([128, 2, 512], BF16)
            sqT = A.alloc([128, 4, 512], BF16)
            kte = A.alloc([128, 256], BF16)
            vtok = A.alloc([128, 512], BF16)
            attTm = A.alloc([128, 512], BF16)
            pTm = A.alloc([128, 2, 512], BF16)
            ss = A.alloc([128, 4], F32)
            den2 = A.alloc([128, 8], F32)
            omix = A.alloc([128, 1024], BF16)
            omixT = A.alloc([128, 8, 512], BF16)

        P.dma("sp", ident[:], ident_in, writes=[("ident",)])
        for i in range(5):
            P.dma("sp", gvec[:, i, :], gvec_in[i], writes=[("gvec",)])
        P.op("dve", lambda e: e.memset(ones_bf[:], 1.0 / D), writes=[("ones",)])
        P.op("dve", lambda e: e.memset(ones1[:], 1.0), writes=[("ones1",)])
        if CROSS:
            P.dma("pool", wq[:], wq_v, writes=[("wq",)])
            P.dma("pool", wo[:], wo_v, writes=[("wo",)])
        if FFN:
            P.dma("sp", convp[:], convp_in, writes=[("convp",)])
            P.op("dve", lambda e: e.memset(halo[:], 0.0), writes=[("halo",)])
        if MIX:
            for i in range(2):
                P.dma("pool", w_in[:, :, i * 1160:(i + 1) * 1160],
                      w_in_v[:, :, i * 1160:(i + 1) * 1160], writes=[("w_in", i)])
            for kvh in range(2):
                for d_ in range(2):
                    P.dma("pool", skdup[:, :, kvh * 128 + d_ * 64: kvh * 128 + d_ * 64 + 64],
                          w_in_v[:, :, 2064 + kvh * 64: 2064 + kvh * 64 + 64],
                          writes=[("skdup", kvh, d_)])
            P.dma("pool", w_out[:], w_out_v, writes=[("w_out",)])
            P.dma("pool", identb[:], ident_in, writes=[("identb",)])
            P.dma("pool", walpha[0:32, :], walpha_in, writes=[("walpha",)])
            P.dma("pool", maskA[:], maskA_in, writes=[("maskA",)])
            P.dma("sp", gain[:], gain_in, writes=[("gain",)])
            P.dma("sp", triLE[:], tri_in[0], writes=[("triLE",)])
            P.dma("sp", triGT[:], tri_in[1], writes=[("triGT",)])
            P.dma("sp", bias_f[:], bias_in, writes=[("@s.bias",)])
            P.dma("sp", negm[:], negm_in, writes=[("@s.negm",)])
            P.dma("sp", sinks_b[:], sinks_in, writes=[("@s.sinks",)])
            for kvh in range(2):
                P.op("dve", lambda e, kvh=kvh: e.tensor_tensor(
                    out=biasm[:, kvh, :, :], in0=bias_f[:, kvh, :, :], in1=negm[:, :, :],
                    op=ALU.add), reads=[("@s.bias",), ("@s.negm",)], writes=[("biasm",)])
            P.op("act", lambda e: e.activation(out=esink[:], in_=sinks_b[:], func=AF.Exp),
                 reads=[("@s.sinks",)], writes=[("esink",)])
            P.op("dve", lambda e: e.memset(S[:], 0.0), writes=[("S",)])
            P.op("dve", lambda e: e.memset(Sbf[:], 0.0), writes=[("Sbf",)])
            P.op("dve", lambda e: e.memset(skT[:], 0.0), writes=[("skT", 0), ("skT", 1)])
            P.op("dve", lambda e: e.memset(vext[:], 1.0), writes=[("vext", 0), ("vext", 1)])
            P.fence()

        W_IN_KEYS = [("w_in", 0), ("w_in", 1)]

        def rms_stats(src_fn, src_keys, ncols, sq, sq_key, out_rstd, out_key):
            b = P.bank()
            for c in range(8):
                s = c % 2
                P.op("act", lambda e, c=c, s=s: e.activation(
                    out=sq[:, s, :ncols], in_=src_fn(c), func=AF.Square),
                    reads=[src_keys[c]], writes=[(sq_key, s)])
                P.op("pe", lambda e, c=c, s=s, b=b: e.matmul(
                    banks[b][:, :ncols], lhsT=ones_bf[:], rhs=sq[:, s, :ncols],
                    start=(c == 0), stop=(c == 7)),
                    reads=[(sq_key, s), ("ones",)], writes=[("ps", b)])
            P.op("act", lambda e, b=b: e.activation(
                out=out_rstd[:, :ncols], in_=banks[b][:, :ncols], func=AF.Ln, bias=float(EPS)),
                reads=[("ps", b)], writes=[out_key])
            P.op("act", lambda e: e.activation(
                out=out_rstd[:, :ncols], in_=out_rstd[:, :ncols], func=AF.Exp, scale=-0.5),
                reads=[out_key], writes=[out_key])

        def hslice(c, g):
            return hT[:, c, g * 512:(g + 1) * 512]

        def hkeys(g):
            return [("hT", c, g) for c in range(8)]

        def evac(i, out_ap, bank_ap, reads, writes, scale=None):
            if i % 2 == 0:
                if scale is None:
                    P.op("act", lambda e: e.copy(out=out_ap, in_=bank_ap), reads=reads,
                         writes=writes)
                else:
                    P.op("act", lambda e: e.mul(out_ap, bank_ap, float(scale)),
                         reads=reads, writes=writes)
            else:
                if scale is None:
                    P.op("dve", lambda e: e.tensor_copy(out=out_ap, in_=bank_ap),
                         reads=reads, writes=writes)
                else:
                    P.op("dve", lambda e: e.tensor_scalar_mul(out_ap, bank_ap, float(scale)),
                         reads=reads, writes=writes)

        def norm_apply(out_fn, out_key_fn, g, gi, rstd_t, rstd_key):
            for c in range(8):
                P.op("dve", lambda e, c=c: e.scalar_tensor_tensor(
                    out=out_fn(c), in0=hslice(c, g), scalar=gvec[:, gi, c:c + 1],
                    in1=rstd_t[:, :], op0=ALU.mult, op1=ALU.mult),
                    reads=[("hT", c, g), rstd_key, ("gvec",)], writes=[out_key_fn(c)])

        def proj_block(lhs_fn, lhs_keys, rhs_fn, rhs_key_fn, out_fn, b):
            for kc in range(8):
                P.op("pe", lambda e, kc=kc: e.matmul(
                    out_fn(), lhsT=lhs_fn(kc), rhs=rhs_fn(kc), start=(kc == 0), stop=(kc == 7)),
                    reads=list(lhs_keys) + [rhs_key_fn(kc)], writes=[("ps", b)])

        def resid_add(b, fo, g):
            P.op("dve", lambda e: e.tensor_tensor(
                out=hslice(fo, g), in0=hslice(fo, g), in1=banks[b][:, :], op=ALU.add),
                reads=[("ps", b), ("hT", fo, g)], writes=[("hT", fo, g)])

        def load_quarter(q):
            for g in range(NQ_TOK // 512):
                tok0 = q * NQ_TOK + g * 512
                for t in range(4):
                    P.dma("sp", xtok[:, t, :], x[tok0 + t * 128: tok0 + (t + 1) * 128, :],
                          writes=[("@l.xtok", t)])
                for c in range(8):
                    b = P.bank()
                    for t in range(4):
                        P.op("pe", lambda e, c=c, t=t, b=b: e.transpose(
                            banks[b][:, t * 128:(t + 1) * 128], xtok[:, t, c * 128:(c + 1) * 128],
                            ident[:]),
                            reads=[("@l.xtok", t), ("ident",)], writes=[("ps", b)])
                    evac(c, hslice(c, g), banks[b][:, :], [("ps", b)], [("hT", c, g)])

        def mix_tile(G, g, t):
            n = G * 4 + t
            par = n % 2
            tsl = slice(t * 128, (t + 1) * 128)
            hn_keys = lambda kc: ("@m.hn", kc)
            lhs_tok = lambda kc: hnT[:, kc, tsl]
            bK = P.bank()
            proj_block(lhs_tok, W_IN_KEYS, lambda kc: w_in[:, kc, 256:512], hn_keys,
                       lambda: banks[bK][:, 0:256], bK)
            proj_block(lhs_tok, W_IN_KEYS, lambda kc: w_in[:, kc, 2192:2320], hn_keys,
                       lambda: banks[bK][:, 256:384], bK)
            bV = P.bank()
            proj_block(lhs_tok, W_IN_KEYS, lambda kc: w_in[:, kc, 512:1024], hn_keys,
                       lambda: banks[bV][:, :], bV)
            bR = P.bank()
            proj_block(lhs_tok, W_IN_KEYS, lambda kc: w_in[:, kc, 1024:1536], hn_keys,
                       lambda: banks[bR][:, :], bR)
            P.op("dve", lambda e: e.tensor_tensor(
                out=kte[:, :], in0=banks[bK][:, 0:256], in1=Eend[:, t * 256:(t + 1) * 256],
                op=ALU.mult), reads=[("ps", bK), ("@m.L", t // 2)], writes=[("@m.kte",)])
            P.op("act", lambda e: e.copy(out=vtok[:, :], in_=banks[bV][:, :]),
                 reads=[("ps", bV)], writes=[("@m.v",)])
            P.op("dve", lambda e: e.tensor_copy(
                out=vext[:, par, :, 0:64],
                in_=banks[bK][:, 256:384].rearrange("p (a d) -> p a d", a=2)),
                reads=[("ps", bK)], writes=[("vext", par)])
            P.op("act", lambda e: e.activation(out=er[:, :], in_=banks[bR][:, :], func=AF.Exp,
                                               scale=-1.0),
                 reads=[("ps", bR)], writes=[("@m.sq", 0), ("@m.sq", 1)])
            P.op("dve", lambda e: e.tensor_scalar_add(er[:, :], er[:, :], 1.0),
                 reads=[("@m.sq", 0), ("@m.sq", 1)], writes=[("@m.sq", 0), ("@m.sq", 1)])
            P.op("dve", lambda e: e.reciprocal(out=er[:, :], in_=er[:, :]),
                 reads=[("@m.sq", 0), ("@m.sq", 1)], writes=[("@m.sq", 0), ("@m.sq", 1)])
            P.op("dve", lambda e: e.tensor_tensor(out=er[:, :], in0=banks[bR][:, :],
                                                  in1=er[:, :], op=ALU.mult),
                 reads=[("ps", bR), ("@m.sq", 0), ("@m.sq", 1)], writes=[("@m.sq", 0), ("@m.sq", 1)])
            if MIX_STOP <= 3:
                return
            bA2 = [P.bank(), P.bank()]
            for h in range(4):
                hp, hb = h % 2, h // 2
                P.op("pe", lambda e, h=h, hp=hp, hb=hb: e.matmul(
                    banks[bA2[hp]][:, hb * 128:(hb + 1) * 128],
                    lhsT=kinv[hp * 64:(hp + 1) * 64, hb, tsl],
                    rhs=qdec[hp * 64:(hp + 1) * 64, hb, tsl], start=True, stop=True),
                    reads=[("@m.kinv", hb), ("@m.qdec", hb)], writes=[("ps", bA2[hp])])
            for hp in range(2):
                P.op("dve", lambda e, hp=hp: e.tensor_tensor(
                    out=attTm[:, :].rearrange("p (hb hp i) -> p hp hb i", hb=2, hp=2)[:, hp],
                    in0=banks[bA2[hp]][:, 0:256].rearrange("p (hb i) -> p hb i", hb=2),
                    in1=maskA[:, :].unsqueeze(1).to_broadcast([128, 2, 128]), op=ALU.mult),
                    reads=[("ps", bA2[hp]), ("maskA",)], writes=[("@m.att",)])
            if MIX_STOP <= 3.3:
                return
            bO2 = [P.bank(), P.bank()]
            for h in range(4):
                hp, hb = h % 2, h // 2
                P.op("pe", lambda e, h=h, hp=hp, hb=hb: e.matmul(
                    banks[bO2[hp]][:, hb * 128:(hb + 1) * 128],
                    lhsT=attTm[:, h * 128:(h + 1) * 128],
                    rhs=vtok[:, h * 128:(h + 1) * 128], start=True, stop=False),
                    reads=[("@m.att",), ("@m.v",)], writes=[("ps", bO2[hp])])
                P.op("pe", lambda e, h=h, hp=hp, hb=hb: e.matmul(
                    banks[bO2[hp]][:, hb * 128:(hb + 1) * 128],
                    lhsT=qdec[hp * 64:(hp + 1) * 64, hb, tsl],
                    rhs=Sbf[hp * 64:(hp + 1) * 64, hb, :], start=False, stop=True),
                    reads=[("@m.qdec", hb), ("Sbf",)], writes=[("ps", bO2[hp])])
            if MIX_STOP <= 3.6:
                return
            bKV = P.bank()
            for h in range(4):
                hp, hb = h % 2, h // 2
                P.op("pe", lambda e, h=h, hp=hp, hb=hb: e.matmul(
                    banks[bKV][hp * 64:(hp + 1) * 64, hb * 128:(hb + 1) * 128],
                    lhsT=kte[:, h * 64:(h + 1) * 64], rhs=vtok[:, h * 128:(h + 1) * 128],
                    start=True, stop=True),
                    reads=[("@m.kte",), ("@m.v",)], writes=[("ps", bKV)])
            if MIX_STOP <= 3.8:
                return
            for hb in range(2):
                P.op("dve", lambda e, hb=hb: e.scalar_tensor_tensor(
                    out=S[:, hb, :], in0=S[:, hb, :],
                    scalar=Ebc[:, hb, t * 128 + 127:t * 128 + 128],
                    in1=banks[bKV][:, hb * 128:(hb + 1) * 128], op0=ALU.mult, op1=ALU.add),
                    reads=[("S",), ("@m.Ebc", hb), ("ps", bKV)], writes=[("S",)])
            P.op("act", lambda e: e.copy(out=Sbf[:], in_=S[:]), reads=[("S",)],
                 writes=[("Sbf",)])
            if MIX_STOP <= 4:
                return
            P.op("dve", lambda e: e.memset(ss[:, :], 0.0), writes=[("@m.ss",)])
            for h in range(4):
                P.op("act", lambda e, h=h: e.activation(
                    out=junk[:, :], in_=banks[bO2[h % 2]][:, (h // 2) * 128:(h // 2 + 1) * 128],
                    func=AF.Square, accum_out=ss[:, h:h + 1]),
                    reads=[("ps", bO2[h % 2]), ("@m.ss",)], writes=[("@m.ss",), ("@m.rstd",)])
            P.op("act", lambda e: e.activation(out=ss[:, :], in_=ss[:, :], func=AF.Ln,
                                               bias=float(EPS), scale=1.0 / 128.0),
                 reads=[("@m.ss",)], writes=[("@m.ss",)])
            P.op("act", lambda e: e.activation(out=ss[:, :], in_=ss[:, :], func=AF.Exp,
                                               scale=-0.5),
                 reads=[("@m.ss",)], writes=[("@m.ss",)])
            P.op("dve", lambda e: e.tensor_tensor(
                out=er[:, :].rearrange("p (h v) -> p h v", h=4),
                in0=er[:, :].rearrange("p (h v) -> p h v", h=4),
                in1=ss[:, :].unsqueeze(2).to_broadcast([128, 4, 128]), op=ALU.mult),
                reads=[("@m.sq", 0), ("@m.sq", 1), ("@m.ss",)], writes=[("@m.sq", 0), ("@m.sq", 1)])
            for hp in range(2):
                P.op("dve", lambda e, hp=hp: e.tensor_tensor(
                    out=omix[:, 0:512].rearrange("p (hb hp v) -> p hp hb v", hb=2, hp=2)[:, hp],
                    in0=banks[bO2[hp]][:, 0:256].rearrange("p (hb v) -> p hb v", hb=2),
                    in1=er[:, :].rearrange("p (hb hp v) -> p hp hb v", hb=2, hp=2)[:, hp],
                    op=ALU.mult),
                    reads=[("ps", bO2[hp]), ("@m.sq", 0), ("@m.sq", 1)], writes=[("@m.omix", 0)])
            if MIX_STOP <= 5:
                return
            kts = [1] if n == 0 else [0, 1]
            for kvh in range(2):
                for kt in kts:
                    koff = 128 + t * 128 if kt == 1 else t * 128
                    for pp in range(2):
                        bs = P.bank()
                        for bb in range(2):
                            P.op("pe", lambda e, bs=bs, koff=koff, kvh=kvh, pp=pp, bb=bb: e.matmul(
                                banks[bs][:, bb * 128:(bb + 1) * 128],
                                lhsT=skT[pp * 64:(pp + 1) * 64, kvh, koff:koff + 128],
                                rhs=sqT[pp * 64:(pp + 1) * 64, 2 * kvh + bb, tsl],
                                start=(bb == 0), stop=(bb == 0), skip_group_check=(bb > 0)),
                                reads=[("skT", kvh), ("@m.sqT", 2 * kvh + bb)],
                                writes=[("ps", bs)])
                        P.op("pe", lambda e, bs=bs, kvh=kvh, kt=kt, pp=pp: e.matmul(
                            banks[bs][:, 0:256], lhsT=identb[:, :],
                            rhs=biasm[:, kvh, kt, pp * 256:(pp + 1) * 256],
                            start=False, stop=True, skip_group_check=True),
                            reads=[("identb",), ("biasm",)], writes=[("ps", bs)])
                        P.op("act", lambda e, bs=bs, kvh=kvh, kt=kt, pp=pp: e.activation(
                            out=pTm[:, kt, :].rearrange(
                                "p (bb pp q) -> p pp bb q", bb=2, pp=2)[:, pp],
                            in_=banks[bs][:, 0:256].rearrange("p (bb q) -> p bb q", bb=2),
                            func=AF.Exp),
                            reads=[("ps", bs)], writes=[("@m.pT", kt)])
                bo = P.bank()
                for j in range(4):
                    for idx, kt in enumerate(kts):
                        pk = par if kt == 1 else 1 - par
                        P.op("pe", lambda e, bo=bo, j=j, kt=kt, pk=pk, idx=idx, kvh=kvh: e.matmul(
                            banks[bo][:, j * 65:(j + 1) * 65],
                            lhsT=pTm[:, kt, j * 128:(j + 1) * 128],
                            rhs=vext[:, pk, kvh, 0:65], start=(idx == 0),
                            stop=(idx == len(kts) - 1)),
                            reads=[("@m.pT", kt), ("vext", pk)], writes=[("ps", bo)])
                ov = banks[bo][:, 0:260].rearrange("p (j e) -> p j e", e=65)
                P.op("dve", lambda e, ov=ov, kvh=kvh: e.tensor_tensor(
                    out=den2[:, kvh * 4:(kvh + 1) * 4], in0=ov[:, :, 64],
                    in1=esink[:, kvh * 4:(kvh + 1) * 4], op=ALU.add),
                    reads=[("ps", bo), ("esink",)], writes=[("@m.den", kvh)])
                P.op("dve", lambda e, kvh=kvh: e.reciprocal(
                    out=den2[:, kvh * 4:(kvh + 1) * 4], in_=den2[:, kvh * 4:(kvh + 1) * 4]),
                    reads=[("@m.den", kvh)], writes=[("@m.den", kvh)])
                P.op("dve", lambda e, ov=ov, kvh=kvh: e.tensor_tensor(
                    out=omix[:, 512 + kvh * 256: 512 + (kvh + 1) * 256].rearrange(
                        "p (j d) -> p j d", d=64),
                    in0=ov[:, :, 0:64],
                    in1=den2[:, kvh * 4:(kvh + 1) * 4].unsqueeze(2).to_broadcast([128, 4, 64]),
                    op=ALU.mult),
                    reads=[("ps", bo), ("@m.den", kvh)], writes=[("@m.omix", 1 + kvh)])
            if MIX_STOP <= 6:
                return
            bT2 = [P.bank(), P.bank()]
            for c in range(8):
                P.op("pe", lambda e, c=c: e.matmul(
                    banks[bT2[c // 4]][:, (c % 4) * 128:(c % 4 + 1) * 128],
                    lhsT=omix[:, c * 128:(c + 1) * 128], rhs=identb[:, :], start=True, stop=True),
                    reads=[("@m.omix", 0 if c < 4 else (1 if c < 6 else 2)), ("identb",)],
                    writes=[("ps", bT2[c // 4])])
            P.op("act", lambda e: e.activation(
                out=omixT[:, 0:4, tsl],
                in_=banks[bT2[0]][:, :].rearrange("p (c k) -> p c k", c=4),
                func=AF.Identity, scale=gain[:, 0:1]),
                reads=[("ps", bT2[0]), ("gain",)], writes=[("@m.omixT", 0)])
            P.op("dve", lambda e: e.tensor_copy(
                out=omixT[:, 4:8, tsl],
                in_=banks[bT2[1]][:, :].rearrange("p (c k) -> p c k", c=4)),
                reads=[("ps", bT2[1])], writes=[("@m.omixT", 1)])

        def mix_group(q, g):
            G = q * 2 + g
            rms_stats(lambda c: hslice(c, g), hkeys(g), 512, m_sq, "@m.sq", m_rstd, ("@m.rstd",))
            norm_apply(lambda c: hnT[:, c, :], lambda c: ("@m.hn", c), g, G_MIX, m_rstd,
                       ("@m.rstd",))
            hn_keys = lambda kc: ("@m.hn", kc)
            rhs_hn = lambda kc: hnT[:, kc, :]
            P.op("dve", lambda e: e.memset(alow[0:32, :], 1.0), writes=[("@m.rstd",)])
            b = P.bank()
            proj_block(lambda kc: w_in[:, kc, 1536:1552], W_IN_KEYS, rhs_hn, hn_keys,
                       lambda b=b: banks[b][0:16, :], b)
            P.op("act", lambda e, b=b: e.copy(out=alow[0:16, :], in_=banks[b][0:16, :]),
                 reads=[("ps", b)], writes=[("@m.rstd",)])
            for b4 in range(4):
                b = P.bank()
                proj_block(lambda kc, b4=b4: w_in[:, kc, 1552 + b4 * 128:1552 + (b4 + 1) * 128],
                           W_IN_KEYS, rhs_hn, hn_keys, lambda b=b: banks[b][:, :], b)
                evac(b4, sqT[:, b4, :], banks[b][:, :], [("ps", b)], [("@m.sqT", b4)],
                     scale=0.125)
            for kvh in range(2):
                b = P.bank()
                proj_block(lambda kc, kvh=kvh: skdup[:, kc, kvh * 128:(kvh + 1) * 128],
                           [("skdup", kvh, 0), ("skdup", kvh, 1)], rhs_hn, hn_keys,
                           lambda b=b: banks[b][:, :], b)
                evac(kvh, skT[:, kvh, 128:640], banks[b][:, :], [("ps", b)], [("skT", kvh)])
            if MIX_STOP <= 1:
                return
            zb = [P.bank(), P.bank()]
            for t in range(4):
                P.op("pe", lambda e, t=t: e.matmul(
                    banks[zb[t // 2]][:, (t % 2) * 256:(t % 2 + 1) * 256],
                    lhsT=alow[0:32, t * 128:(t + 1) * 128], rhs=walpha[0:32, :],
                    start=True, stop=True),
                    reads=[("@m.rstd",), ("walpha",)], writes=[("ps", zb[t // 2])])
            for i in range(2):
                P.op("act", lambda e, i=i: e.activation(
                    out=L[:, i * 512:(i + 1) * 512], in_=banks[zb[i]][:, :], func=AF.Exp,
                    scale=-1.0), reads=[("ps", zb[i])], writes=[("@m.L", i)])
                P.op("act", lambda e, i=i: e.activation(
                    out=L[:, i * 512:(i + 1) * 512], in_=L[:, i * 512:(i + 1) * 512],
                    func=AF.Ln, bias=1.0), reads=[("@m.L", i)], writes=[("@m.L", i)])
            cb = [P.bank(), P.bank()]
            for hb in range(2):
                for t in range(4):
                    P.op("pe", lambda e, hb=hb, t=t: e.matmul(
                        banks[cb[hb]][:, t * 128:(t + 1) * 128],
                        lhsT=L[:, t * 256 + hb * 128: t * 256 + (hb + 1) * 128],
                        rhs=triLE[:, :], start=True, stop=True),
                        reads=[("@m.L", t // 2), ("triLE",)], writes=[("ps", cb[hb])])
            te = [P.bank(), P.bank()]
            for t in range(4):
                P.op("pe", lambda e, t=t: e.matmul(
                    banks[te[t // 2]][:, (t % 2) * 256:(t % 2 + 1) * 256],
                    lhsT=triGT[:, :], rhs=L[:, t * 256:(t + 1) * 256], start=True, stop=True),
                    reads=[("@m.L", t // 2), ("triGT",)], writes=[("ps", te[t // 2])])
            for hb in range(2):
                P.op("act", lambda e, hb=hb: e.activation(
                    out=Ebc[:, hb, :], in_=banks[cb[hb]][:, :], func=AF.Exp),
                    reads=[("ps", cb[hb])], writes=[("@m.Ebc", hb)])
                P.op("act", lambda e, hb=hb: e.activation(
                    out=Einv[:, hb, :], in_=banks[cb[hb]][:, :], func=AF.Exp, scale=-1.0),
                    reads=[("ps", cb[hb])], writes=[("@m.Einv", hb)])
            for i in range(2):
                P.op("act", lambda e, i=i: e.activation(
                    out=Eend[:, i * 512:(i + 1) * 512], in_=banks[te[i]][:, :], func=AF.Exp),
                    reads=[("ps", te[i])], writes=[("@m.L", i)])
            for hb in range(2):
                b = P.bank()
                proj_block(lambda kc, hb=hb: w_in[:, kc, hb * 128:(hb + 1) * 128], W_IN_KEYS,
                           rhs_hn, hn_keys, lambda b=b: banks[b][:, :], b)
                P.op("dve", lambda e, b=b, hb=hb: e.scalar_tensor_tensor(
                    out=qdec[:, hb, :], in0=banks[b][:, :], scalar=0.125, in1=Ebc[:, hb, :],
                    op0=ALU.mult, op1=ALU.mult),
                    reads=[("ps", b), ("@m.Ebc", hb)], writes=[("@m.qdec", hb)])
            for hb in range(2):
                b = P.bank()
                proj_block(lambda kc, hb=hb: w_in[:, kc, 256 + hb * 128:256 + (hb + 1) * 128],
                           W_IN_KEYS, rhs_hn, hn_keys, lambda b=b: banks[b][:, :], b)
                P.op("dve", lambda e, b=b, hb=hb: e.tensor_tensor(
                    out=kinv[:, hb, :], in0=banks[b][:, :], in1=Einv[:, hb, :], op=ALU.mult),
                    reads=[("ps", b), ("@m.Einv", hb)], writes=[("@m.kinv", hb)])
            if MIX_STOP <= 2:
                return
            for t in range(4):
                mix_tile(G, g, t)
            if MIX_STOP <= 7:
                return
            for fo in range(8):
                b = P.bank()
                proj_block(lambda kc, fo=fo: w_out[:, kc, fo * 128:(fo + 1) * 128], [("w_out",)],
                           lambda kc: omixT[:, kc, :], lambda kc: ("@m.omixT", kc // 4),
                           lambda b=b: banks[b][:, :], b)
                resid_add(b, fo, g)
            P.op("dve", lambda e: e.tensor_copy(out=skT[:, :, 0:128], in_=skT[:, :, 512:640]),
                 reads=[("skT", 0), ("skT", 1)], writes=[("skT", 0), ("skT", 1)])

        def mem_kv():
            for t in range(2):
                P.dma("sp", k_xtok[:, t, :], mem[t * 128:(t + 1) * 128, :],
                      writes=[("@k.xtok", t)])
            P.dma("pool", wk[:], wk_v, writes=[("@k.wk",)])
            for c in range(8):
                b = P.bank()
                for t in range(2):
                    P.op("pe", lambda e, c=c, t=t, b=b: e.transpose(
                        banks[b][:, t * 128:(t + 1) * 128], k_xtok[:, t, c * 128:(c + 1) * 128],
                        ident[:]),
                        reads=[("@k.xtok", t), ("ident",)], writes=[("ps", b)])
                evac(c, memT[:, c, :], banks[b][:, 0:256], [("ps", b)], [("@k.memT", c)])
            rms_stats(lambda c: memT[:, c, :], [("@k.memT", c) for c in range(8)], 256,
                      k_sq, "@k.sq", k_rstd, ("@k.rstd",))
            for c in range(8):
                P.op("dve", lambda e, c=c: e.scalar_tensor_tensor(
                    out=memn[:, c, :], in0=memT[:, c, :], scalar=gvec[:, G_MEM, c:c + 1],
                    in1=k_rstd[:, 0:256], op0=ALU.mult, op1=ALU.mult),
                    reads=[("@k.memT", c), ("@k.rstd",), ("gvec",)], writes=[("@k.memn", c)])
            for dc in range(8):
                b = P.bank()
                proj_block(lambda kc, dc=dc: wk[:, kc, dc * 128:(dc + 1) * 128], [("@k.wk",)],
                           lambda kc: memn[:, kc, :], lambda kc: ("@k.memn", kc),
                           lambda b=b: banks[b][:, 0:256], b)
                evac(dc, KT[:, dc, :], banks[b][:, 0:256], [("ps", b)], [("KT",)])
            P.dma("pool", wv[:], wv_v, writes=[("@k.wk",)])
            for mt in range(2):
                for half in range(2):
                    b = P.bank()
                    proj_block(lambda kc, mt=mt: memn[:, kc, mt * 128:(mt + 1) * 128],
                               [("@k.wk",)],
                               lambda kc, half=half: wv[:, kc, half * 512:(half + 1) * 512],
                               lambda kc: ("@k.memn", kc), lambda b=b: banks[b][:, :], b)
                    evac(half, Vm[:, mt, half * 512:(half + 1) * 512], banks[b][:, :],
                         [("ps", b)], [("Vm",)])

        def cross_head(h, hp):
            for mt in range(2):
                b = P.bank()
                for d2 in range(2):
                    dc = 2 * h + d2
                    P.op("pe", lambda e, b=b, dc=dc, d2=d2, mt=mt: e.matmul(
                        banks[b][:, :], lhsT=KT[:, dc, mt * 128:(mt + 1) * 128], rhs=cq[:, dc, :],
                        start=(d2 == 0), stop=(d2 == 1)),
                        reads=[("KT",), ("@b.cq", dc)], writes=[("ps", b)])
                P.op("act", lambda e, b=b, mt=mt: e.activation(
                    out=pT[:, hp, mt, :], in_=banks[b][:, :], func=AF.Exp),
                    reads=[("ps", b)], writes=[("@b.pT", hp, mt)])
            b = P.bank()
            for mt in range(2):
                P.op("pe", lambda e, b=b, mt=mt: e.matmul(
                    banks[b][:, :], lhsT=ones1[:], rhs=pT[:, hp, mt, :],
                    start=(mt == 0), stop=(mt == 1)),
                    reads=[("ones1",), ("@b.pT", hp, mt)], writes=[("ps", b)])
            P.op("dve", lambda e, b=b: e.reciprocal(out=rden[:, hp, :], in_=banks[b][:, :]),
                 reads=[("ps", b)], writes=[("@b.rden", hp)])
            for d2 in range(2):
                dc = 2 * h + d2
                b = P.bank()
                for mt in range(2):
                    P.op("pe", lambda e, b=b, dc=dc, mt=mt: e.matmul(
                        banks[b][:, :], lhsT=Vm[:, mt, dc * 128:(dc + 1) * 128],
                        rhs=pT[:, hp, mt, :], start=(mt == 0), stop=(mt == 1)),
                        reads=[("Vm",), ("@b.pT", hp, mt)], writes=[("ps", b)])
                P.op("dve", lambda e, b=b, dc=dc: e.tensor_tensor(
                    out=onT[:, dc, :], in0=banks[b][:, :], in1=rden[:, hp, :], op=ALU.mult),
                    reads=[("ps", b), ("@b.rden", hp)], writes=[("@b.onT", dc)])

        def cross(q):
            for g in range(2):
                rms_stats(lambda c, g=g: hslice(c, g), hkeys(g), 512, b_sq, "@b.sq", b_rstd,
                          ("@b.rstd",))
                norm_apply(lambda c: hn2[:, c, :], lambda c: ("@b.hn2", c), g, G_CROSS, b_rstd,
                           ("@b.rstd",))
                for dc in range(8):
                    b = P.bank()
                    proj_block(lambda kc, dc=dc: wq[:, kc, dc * 128:(dc + 1) * 128], [("wq",)],
                               lambda kc: hn2[:, kc, :], lambda kc: ("@b.hn2", kc),
                               lambda b=b: banks[b][:, :], b)
                    evac(dc, cq[:, dc, :], banks[b][:, :], [("ps", b)], [("@b.cq", dc)],
                         scale=1.0 / 16.0)
                for h in range(4):
                    cross_head(h, h % 2)
                for fo in range(8):
                    b = P.bank()
                    proj_block(lambda kc, fo=fo: wo[:, kc, fo * 128:(fo + 1) * 128], [("wo",)],
                               lambda kc: onT[:, kc, :], lambda kc: ("@b.onT", kc),
                               lambda b=b: banks[b][:, :], b)
                    resid_add(b, fo, g)

        def ffn_load_wup(j):
            slot = j % 3
            for gv in range(2):
                P.dma("pool", wup[:, slot, :, gv, :], w_up_v[:, :, gv, j, :],
                      writes=[("wup", slot, gv)])

        def ffn_load_wdn(w):
            a, bnd = WAVES[w]
            P.dma("pool", wdn[:, 0:bnd - a, :], w_down_v[:, a:bnd, :], writes=[("wdn",)])

        def ffn_prefetch(q):
            ffn_load_wup(0)
            ffn_load_wup(1)

        ffn_cnt = [0]

        def ffn_sub(j, jl, wbuf, slot, e0, n, ts):
            bks = []
            for gv in range(2):
                b = P.bank()
                bks.append(b)
                for kc in range(8):
                    P.op("pe", lambda e, b=b, kc=kc, gv=gv: e.matmul(
                        banks[b][:, :n + 2], lhsT=wup[:, slot, kc, gv, :],
                        rhs=hn3[:, kc, e0:e0 + n + 2], start=(kc == 0), stop=(kc == 7)),
                        reads=[("wup", slot, gv), ("@c.hn3", kc)], writes=[("ps", b)])
            for gv in range(2):
                jj = j + 22 * gv
                b = bks[gv]
                P.op("act", lambda e, b=b, jj=jj, gv=gv: e.activation(
                    out=tA[:, ts, gv, :n], in_=banks[b][:, 2:n + 2], func=AF.Identity,
                    bias=convp[:, jj, 3:4], scale=convp[:, jj, 2:3]),
                    reads=[("ps", b), ("convp",)], writes=[("@c.tA", ts, gv)])
                P.op("dve", lambda e, b=b, jj=jj, gv=gv: e.scalar_tensor_tensor(
                    out=tB[:, ts, gv, :n], in0=banks[b][:, 1:n + 1], scalar=convp[:, jj, 1:2],
                    in1=tA[:, ts, gv, :n], op0=ALU.mult, op1=ALU.add),
                    reads=[("ps", b), ("@c.tA", ts, gv), ("convp",)], writes=[("@c.tB", ts, gv)])
                P.op("dve", lambda e, b=b, jj=jj, gv=gv: e.scalar_tensor_tensor(
                    out=tA[:, ts, gv, :n], in0=banks[b][:, 0:n], scalar=convp[:, jj, 0:1],
                    in1=tB[:, ts, gv, :n], op0=ALU.mult, op1=ALU.add),
                    reads=[("ps", b), ("@c.tB", ts, gv), ("convp",)], writes=[("@c.tA", ts, gv)])
            P.op("act", lambda e: e.activation(
                out=tB[:, ts, 0, :n], in_=tA[:, ts, 0, :n], func=AF.Silu),
                reads=[("@c.tA", ts, 0)], writes=[("@c.tB", ts, 0)])
            P.op("dve", lambda e: e.tensor_tensor(
                out=actT[:, wbuf, jl, e0:e0 + n], in0=tB[:, ts, 0, :n], in1=tA[:, ts, 1, :n],
                op=ALU.mult),
                reads=[("@c.tB", ts, 0), ("@c.tA", ts, 1)], writes=[("@c.actT", wbuf, jl)])

        def ffn_pair(j, jl, wbuf):
            for (e0, n) in SUBS:
                ts = ffn_cnt[0] % 2
                ffn_cnt[0] += 1
                ffn_sub(j, jl, wbuf, j % 3, e0, n, ts)

        def ffn_down(w):
            a, bnd = WAVES[w]
            wbuf = w % 2
            cnt = bnd - a
            for fo in range(8):
                for s2 in range(2):
                    b = P.bank()
                    for jl in range(cnt):
                        P.op("pe", lambda e, b=b, jl=jl, fo=fo, s2=s2: e.matmul(
                            banks[b][:, :], lhsT=wdn[:, jl, fo * 128:(fo + 1) * 128],
                            rhs=actT[:, wbuf, jl, s2 * 512:(s2 + 1) * 512],
                            start=(jl == 0), stop=(jl == cnt - 1)),
                            reads=[("wdn",), ("@c.actT", wbuf, jl)], writes=[("ps", b)])
                    resid_add(b, fo, s2)

        def ffn(q):
            hn3_keys = [("@c.hn3", c) for c in range(8)]
            P.op("dve", lambda e: e.tensor_copy(out=hn3[:, :, 0:2], in_=halo[:, :, :]),
                 reads=[("halo",)], writes=hn3_keys)
            for g in range(2):
                rms_stats(lambda c, g=g: hslice(c, g), hkeys(g), 512, c_sq, "@c.sq", c_rstd,
                          ("@c.rstd",))
                norm_apply(lambda c, g=g: hn3[:, c, 2 + g * 512: 2 + (g + 1) * 512],
                           lambda c: ("@c.hn3", c), g, G_FFN, c_rstd, ("@c.rstd",))
            P.op("dve", lambda e: e.tensor_copy(out=halo[:, :, :],
                                                in_=hn3[:, :, NQ_TOK:NQ_TOK + 2]),
                 reads=hn3_keys, writes=[("halo",)])
            ffn_load_wdn(0)
            for w, (a, bnd) in enumerate(WAVES):
                for j in range(a, bnd):
                    if j + 2 < 22:
                        ffn_load_wup(j + 2)
                    ffn_pair(j, j - a, w % 2)
                    if j == a and w > 0:
                        ffn_down(w - 1)
                        ffn_load_wdn(w)
            ffn_down(len(WAVES) - 1)

        def final(q):
            for g in range(NQ_TOK // 512):
                tok0 = q * NQ_TOK + g * 512
                rms_stats(lambda c, g=g: hslice(c, g), hkeys(g), 512, f_sq, "@f.sq", f_rstd,
                          ("@f.rstd",))
                norm_apply(lambda c: yT[:, c, :], lambda c: ("@f.yT", c), g, G_FIN, f_rstd,
                           ("@f.rstd",))
                for t in range(4):
                    s = t % 2
                    for half in range(2):
                        b = P.bank()
                        for cc in range(4):
                            c = half * 4 + cc
                            P.op("pe", lambda e, c=c, cc=cc, t=t, b=b: e.transpose(
                                banks[b][:, cc * 128:(cc + 1) * 128],
                                yT[:, c, t * 128:(t + 1) * 128], ident[:]),
                                reads=[("@f.yT", c), ("ident",)], writes=[("ps", b)])
                        evac(half, ytok[:, s, half * 512:(half + 1) * 512], banks[b][:, :],
                             [("ps", b)], [("@f.ytok", s, half)])
                    P.dma("sp", y[tok0 + t * 128: tok0 + (t + 1) * 128, :], ytok[:, s, :],
                          reads=[("@f.ytok", s, 0), ("@f.ytok", s, 1)])

        if CROSS:
            mem_kv()
            P.fence()
        for q in range(NQ):
            if FFN:
                ffn_prefetch(q)
            load_quarter(q)
            P.fence()
            if MIX:
                for g in range(2):
                    mix_group(q, g)
                P.fence()
            if CROSS:
                cross(q)
                P.fence()
            if FFN:
                ffn(q)
                P.fence()
            final(q)
            P.fence()

        with nc.Block() as block:
            P.emit(sems, dsems, {"sp": block.sync, "pe": block.tensor, "act": block.scalar,
                                 "dve": block.vector, "pool": block.gpsimd})
    return nc


def _fm(v):
    return np.ascontiguousarray(np.asarray(v, np.float32).reshape(8, 128).T)


def _t5_bucket(dist):
    n = np.maximum(dist, 0)
    max_exact = 16
    nf = np.maximum(n, 1).astype(np.float32)
    large = max_exact + (np.log(nf / max_exact) / math.log(128 / max_exact)
                         * (32 - max_exact)).astype(np.int32)
    large = np.minimum(large, 31)
    return np.where(n < max_exact, n, large)


def host_consts(inputs, phases):
    c = {}
    c["ident"] = np.eye(128, dtype=np.float32)
    z8 = np.zeros(1024, np.float32)
    gv = [inputs["norm_final"], inputs["norm_ffn"][0], inputs["norm_cross"][0],
          inputs["norm_mem"][0], inputs["norm_mix"][0]]
    c["gvec"] = np.ascontiguousarray(np.stack([_fm(v) for v in gv], axis=0))
    if "cross" in phases:
        for k in ("w_q_c", "w_k_c", "w_v_c", "w_o_c"):
            c[k] = np.ascontiguousarray(inputs[k][0], dtype=np.float32)
    if "ffn" in phases:
        cw = np.asarray(inputs["conv_w"][0], np.float32)
        cb = np.asarray(inputs["conv_b"][0], np.float32)
        cp = np.concatenate([cw, cb[None]], axis=0)
        c["convp"] = np.ascontiguousarray(cp.reshape(4, 44, 128).transpose(2, 1, 0))
        c["w_up"] = np.ascontiguousarray(inputs["w_up"][0], dtype=np.float32)
        c["w_down"] = np.ascontiguousarray(inputs["w_down"][0], dtype=np.float32)
    if "mix" in phases:
        c["w_in"] = np.ascontiguousarray(inputs["w_in"][0], dtype=np.float32)
        c["w_out"] = np.ascontiguousarray(inputs["w_out"][0], dtype=np.float32)
        wa = np.zeros((32, 256), np.float32)
        wa[0:16] = np.asarray(inputs["w_alpha"][0], np.float32)
        wa[16] = np.asarray(inputs["b_alpha"][0], np.float32)
        c["walpha"] = wa
        c["gain"] = np.ascontiguousarray(
            np.asarray(inputs["gla_gain"][0], np.float32).reshape(128, 1))
        jj, ii = np.meshgrid(np.arange(128), np.arange(128), indexing="ij")
        causal = (jj <= ii).astype(np.float32)
        c["maskA"] = np.ascontiguousarray(causal)
        tri = np.zeros((2, 128, 128), np.float32)
        tri[0] = np.where(jj <= ii, -1.0 / 16.0, 0.0)
        tri[1] = np.where(jj > ii, -1.0 / 16.0, 0.0)
        c["tri"] = tri
        rel = np.asarray(inputs["rel_bias"], np.float32)
        kk, qq = np.meshgrid(np.arange(128), np.arange(128), indexing="ij")
        bias = np.zeros((128, 2, 2, 2, 2, 128), np.float32)
        negm = np.zeros((128, 2, 4, 128), np.float32)
        for kt in range(2):
            dist = (qq + 128 - kk) if kt == 0 else (qq - kk)
            ok = (dist >= 0) & (dist < 128)
            bk = _t5_bucket(dist)
            for kvh in range(2):
                for g in range(4):
                    bias[:, kvh, kt, g % 2, g // 2, :] = rel[bk, kvh * 4 + g]
            negm[:, kt, :, :] = np.where(ok, 0.0, -30000.0)[:, None, :]
        c["swabias"] = np.ascontiguousarray(bias.reshape(128, 2, 2, 512))
        c["negmask"] = np.ascontiguousarray(negm.reshape(128, 2, 512))
        c["sinks_b"] = np.ascontiguousarray(
            np.broadcast_to(np.asarray(inputs["sinks"][0], np.float32)[None, :], (128, 8)))
    return c


_NC_CACHE = {}
PHASES = ("mix", "cross", "ffn")
DEBUG = False
LAST_RES = None


def kernel(**inputs):
    x = np.asarray(inputs["x"], dtype=np.float32)
    B, T, _ = x.shape
    phases = tuple(PHASES)
    if (T, phases) not in _NC_CACHE:
        _NC_CACHE[(T, phases)] = build(T, phases)
    nc = _NC_CACHE[(T, phases)]
    c = host_consts({k: np.asarray(v) for k, v in inputs.items()}, phases)
    in_maps = []
    for b in range(B):
        m = dict(c)
        m["x"] = np.ascontiguousarray(x[b])
        if "cross" in phases:
            m["mem"] = np.ascontiguousarray(np.asarray(inputs["mem"], np.float32)[b])
        in_maps.append(m)
    res = run_bass_kernel_spmd(nc, in_maps, core_ids=list(range(B)))
    global LAST_RES
    LAST_RES = res.results
    return np.stack([np.asarray(r["y"], dtype=np.float32) for r in res.results], axis=0)
```

```python
import math
from contextlib import ExitStack

import numpy as np
import concourse.bass as bass
import concourse.mybir as mybir
from concourse.bass_utils import run_bass_kernel_spmd

F32 = mybir.dt.float32
BF16 = mybir.dt.bfloat16
AF = mybir.ActivationFunctionType
ALU = mybir.AluOpType
AX = mybir.AxisListType

COMPUTE = ("pe", "act", "dve", "pool")
N_DMA_SEMS = 6


class Prog:
    def __init__(self, nc):
        self.nc = nc
        self.streams = {e: [] for e in ("pe", "act", "dve", "pool", "sp")}
        self.last_w = {}
        self.readers = {}
        self.dma_cnt = {"sp": 0, "pool": 0}
        self.bank_rr = 0
        self.arena_toks = []
        self.fence_deps = []
        self.fenced = set()

    @staticmethod
    def _is_arena(k):
        return isinstance(k[0], str) and k[0].startswith("@")

    @staticmethod
    def _compress(toks):
        best = {}
        for t in toks:
            if t[0] == "c":
                key, v = ("c", t[1]), t[2]
            else:
                key, v = ("d", t[1], t[2]), t[3]
            if v > best.get(key, -1):
                best[key] = v
        return [(k + (v,)) for k, v in best.items()]

    def fence(self):
        self.fence_deps = self._compress(self.arena_toks + self.fence_deps)
        self.arena_toks = []
        self.fenced = set()

    def _deps(self, reads, writes):
        deps = []
        for k in list(reads) + list(writes):
            if self._is_arena(k) and k not in self.fenced:
                deps.extend(self.fence_deps)
                self.fenced.add(k)
        for r in reads:
            if r in self.last_w:
                deps.append(self.last_w[r])
        for w in writes:
            if w in self.last_w:
                deps.append(self.last_w[w])
            deps.extend(self.readers.get(w, ()))
        return deps

    def _commit(self, tok, reads, writes):
        if any(self._is_arena(k) for k in list(reads) + list(writes)):
            self.arena_toks.append(tok)
            if len(self.arena_toks) > 8192:
                self.arena_toks = self._compress(self.arena_toks)
        for r in reads:
            self.readers.setdefault(r, []).append(tok)
        for w in writes:
            self.last_w[w] = tok
            self.readers[w] = []

    def op(self, eng, fn, reads=(), writes=()):
        st = self.streams[eng]
        idx = len(st)
        tok = ("c", eng, idx)
        deps = self._deps(reads, writes)
        st.append(dict(kind="op", fn=fn, deps=deps, tok=tok))
        self._commit(tok, reads, writes)
        return tok

    def dma(self, q, out, in_, reads=(), writes=(), **kw):
        st = self.streams[q]
        n = self.dma_cnt[q]
        self.dma_cnt[q] = n + 1
        semi = n % N_DMA_SEMS
        val = 16 * (n // N_DMA_SEMS + 1)
        tok = ("d", q, semi, val)
        deps = self._deps(reads, writes)
        if n >= N_DMA_SEMS:
            deps.append(("d", q, semi, val - 16))
        st.append(dict(kind="dma", out=out, in_=in_, deps=deps, tok=tok, kw=kw))
        self._commit(tok, reads, writes)
        return tok

    def bank(self):
        b = self.bank_rr
        self.bank_rr = (b + 1) % 8
        return b

    def emit(self, sems, dsems, engines):
        needed = {e: set() for e in COMPUTE}
        for e, st in self.streams.items():
            for o in st:
                for d in o["deps"]:
                    if d[0] == "c":
                        if d[1] == e and e in ("pe", "sp"):
                            continue
                        needed[d[1]].add(d[2])
        sigval = {}
        for e in COMPUTE:
            for rank, idx in enumerate(sorted(needed[e])):
                sigval[(e, idx)] = rank + 1

        def run(e, eng):
            waited = {}
            st = self.streams[e]
            for o in st:
                want = {}
                for d in o["deps"]:
                    if d[0] == "c":
                        if d[1] == e and e in ("pe", "sp"):
                            continue
                        key = ("c", d[1])
                        v = sigval[(d[1], d[2])]
                    else:
                        key = ("d", d[1], d[2])
                        v = d[3]
                    if v > waited.get(key, 0):
                        want[key] = max(want.get(key, 0), v)
                for key, v in want.items():
                    sem = sems[key[1]] if key[0] == "c" else dsems[key[1]][key[2]]
                    eng.wait_ge(sem, v)
                    waited[key] = v
                if o["kind"] == "op":
                    ins = o["fn"](eng)
                    t = o["tok"]
                    if (t[1], t[2]) in sigval:
                        ins.then_inc(sems[e], 1)
                else:
                    t = o["tok"]
                    eng.dma_start(out=o["out"], in_=o["in_"], **o["kw"]).then_inc(
                        dsems[t[1]][t[2]], 16)
            if e in ("sp", "pool"):
                n = self.dma_cnt[e]
                for semi in range(min(n, N_DMA_SEMS)):
                    cnt = (n - 1 - semi) // N_DMA_SEMS + 1
                    eng.wait_ge(dsems[e][semi], 16 * cnt)

        for e, eng_deco in engines.items():
            eng_deco(lambda eng, e=e: run(e, eng))


D = 1024
NQ_TOK = 1024
EPS = 1e-6
WAVES = [(0, 4), (4, 8), (8, 12), (12, 16), (16, 20), (20, 22)]
SUBS = [(0, 342), (342, 342), (684, 340)]
D_FF = 2816
D_IN = 2320
MIX_STOP = 99
ARENA_WORDS = 12488


class Arena:
    def __init__(self, t, words):
        self.t = t
        self.words = words
        self.off = 0

    def reset(self):
        self.off = 0

    def alloc(self, shape, dt):
        n = 1
        for s_ in shape[1:]:
            n *= s_
        esz = 4 if dt == F32 else 2
        words = (n * esz + 3) // 4
        words = (words + 1) // 2 * 2
        a = self.t[:, self.off:self.off + words]
        self.off += words
        assert self.off <= self.words, ("arena overflow", self.off, self.words)
        if dt != F32:
            a = a.bitcast(dt)
        a = a[:, 0:n]
        if len(shape) > 2:
            names = "abcd"[:len(shape) - 1]
            pat = "p (" + " ".join(names) + ") -> p " + " ".join(names)
            a = a.rearrange(pat, **{nm: int(sz) for nm, sz in zip(names, shape[1:])})
        return a


def build(T, phases=("mix", "cross", "ffn")):
    nc = bass.Bass("TRN2", target_bir_lowering=False)
    MIX, CROSS, FFN = ("mix" in phases), ("cross" in phases), ("ffn" in phases)

    def din(name, shape):
        return nc.dram_tensor(name, list(shape), F32, kind="ExternalInput").ap()

    x = din("x", [T, D])
    gvec_in = din("gvec", [5, 128, 8])
    ident_in = din("ident", [128, 128])
    y = nc.dram_tensor("y", [T, D], F32, kind="ExternalOutput").ap()
    if FFN:
        convp_in = din("convp", [128, 44, 4])
        w_up = din("w_up", [D, 2 * D_FF])
        w_down = din("w_down", [D_FF, D])
        w_up_v = w_up.rearrange("(kc p) (gv j n) -> p kc gv j n", p=128, gv=2, n=128)
        w_down_v = w_down.rearrange("(j p) n -> p j n", p=128)
    if CROSS:
        mem = din("mem", [256, D])
        wq_v = din("w_q_c", [D, D]).rearrange("(kc p) n -> p kc n", p=128)
        wk_v = din("w_k_c", [D, D]).rearrange("(kc p) n -> p kc n", p=128)
        wv_v = din("w_v_c", [D, D]).rearrange("(kc p) n -> p kc n", p=128)
        wo_v = din("w_o_c", [D, D]).rearrange("(kc p) n -> p kc n", p=128)
    if MIX:
        w_in_v = din("w_in", [D, D_IN]).rearrange("(kc p) n -> p kc n", p=128)
        w_out_v = din("w_out", [D, D]).rearrange("(kc p) n -> p kc n", p=128)
        walpha_in = din("walpha", [32, 256])
        gain_in = din("gain", [128, 1])
        maskA_in = din("maskA", [128, 128])
        tri_in = din("tri", [2, 128, 128])
        bias_in = din("swabias", [128, 2, 2, 512])
        negm_in = din("negmask", [128, 2, 512])
        sinks_in = din("sinks_b", [128, 8])

    NQ = T // NQ_TOK
    with ExitStack() as es:
        def sb(name, shape, dt):
            return es.enter_context(nc.sbuf_tensor(name, shape, dt))

        hT = sb("hT", [128, 8, NQ_TOK], F32)
        ident = sb("identf", [128, 128], F32)
        ones_bf = sb("ones_bf", [128, 128], BF16)
        ones1 = sb("ones1", [128, 128], BF16)
        gvec = sb("gvec_sb", [128, 5, 8], F32)
        G_FIN, G_FFN, G_CROSS, G_MEM, G_MIX = range(5)
        if FFN:
            convp = sb("convp_sb", [128, 44, 4], F32)
            wup = sb("wup", [128, 3, 8, 2, 128], BF16)
            wdn = sb("wdn", [128, 4, 1024], BF16)
            halo = sb("halo", [128, 8, 2], BF16)
        if CROSS:
            wq = sb("wq", [128, 8, 1024], BF16)
            wo = sb("wo", [128, 8, 1024], BF16)
            KT = sb("KT", [128, 8, 256], BF16)
            Vm = sb("Vm", [128, 2, 1024], BF16)
        if MIX:
            w_in = sb("w_in_sb", [128, 8, 2192], BF16)
            skdup = sb("skdup", [128, 8, 256], BF16)
            w_out = sb("w_out_sb", [128, 8, 1024], BF16)
            identb = sb("identb", [128, 128], BF16)
            walpha = sb("walpha_sb", [128, 256], BF16)
            gain = sb("gain_sb", [128, 1], F32)
            maskA = sb("maskA_sb", [128, 128], BF16)
            triLE = sb("triLE", [128, 128], F32)
            triGT = sb("triGT", [128, 128], F32)
            biasm = sb("biasm", [128, 2, 2, 512], BF16)
            esink = sb("esink", [128, 8], F32)
            S = sb("S_state", [128, 2, 128], F32)
            Sbf = sb("S_bf", [128, 2, 128], BF16)
            skT = sb("skT", [128, 2, 640], BF16)
            vext = sb("vext", [128, 2, 2, 65], BF16)
        arena_t = sb("arena", [128, ARENA_WORDS], F32)
        A = Arena(arena_t, ARENA_WORDS)

        banks = [es.enter_context(nc.psum_tensor(f"bank{i}", [128, 512], F32)) for i in range(8)]
        banks_bf = [bk.bitcast(BF16) for bk in banks]
        sems = {e: es.enter_context(nc.semaphore(f"s_{e}")) for e in COMPUTE}
        dsems = {q: [es.enter_context(nc.semaphore(f"d_{q}{i}")) for i in range(N_DMA_SEMS)]
                 for q in ("sp", "pool")}

        P = Prog(nc)

        A.reset()
        xtok = A.alloc([128, 4, 1024], F32)
        A.reset()
        f_sq = A.alloc([128, 2, 512], BF16)
        f_rstd = A.alloc([128, 512], F32)
        ytok = A.alloc([128, 2, 1024], F32)
        yT = A.alloc([128, 8, 512], F32)
        if FFN:
            A.reset()
            c_sq = A.alloc([128, 2, 512], BF16)
            c_rstd = A.alloc([128, 512], F32)
            hn3 = A.alloc([128, 8, NQ_TOK + 2], BF16)
            actT = A.alloc([128, 2, 4, NQ_TOK], BF16)
            tA = A.alloc([128, 2, 2, 344], F32)
            tB = A.alloc([128, 2, 2, 344], F32)
        if CROSS:
            A.reset()
            b_sq = A.alloc([128, 2, 512], BF16)
            b_rstd = A.alloc([128, 512], F32)
            hn2 = A.alloc([128, 8, 512], BF16)
            cq = A.alloc([128, 8, 512], BF16)
            pT = A.alloc([128, 2, 2, 512], BF16)
            rden = A.alloc([128, 2, 512], F32)
            onT = A.alloc([128, 8, 512], BF16)
            A.reset()
            k_sq = A.alloc([128, 2, 512], BF16)
            k_rstd = A.alloc([128, 512], F32)
            k_xtok = A.alloc([128, 2, 1024], F32)
            memT = A.alloc([128, 8, 256], F32)
            memn = A.alloc([128, 8, 256], BF16)
            wk = A.alloc([128, 8, 1024], BF16)
            wv = wk
        if MIX:
            A.reset()
            bias_f = A.alloc([128, 2, 2, 512], F32)
            negm = A.alloc([128, 2, 512], F32)
            sinks_b = A.alloc([128, 8], F32)
            A.reset()
            er = A.alloc([128, 512], F32)
            m_sq = er.bitcast(BF16).rearrange("p (a b) -> p a b", a=2)
            m_rstd = A.alloc([128, 512], F32)
            alow = m_rstd.bitcast(BF16)[:, 0:512]
            junk = m_rstd[:, 256:384]
            hnT = A.alloc([128, 8, 512], BF16)
            L = A.alloc([128, 1024], F32)
            Eend = L
            Ebc = A.alloc([128, 2, 512], F32)
            Einv = A.alloc([128, 2, 512], F32)
            qdec = A.alloc([128, 2, 512], BF16)
            kinv = A.alloc([128, 2, 512], BF16)
            sqT = A.alloc([128, 4, 512], BF16)
            kte = A.alloc([128, 256], BF16)
            vtok = A.alloc([128, 512], BF16)
            attTm = A.alloc([128, 512], BF16)
            pTm = A.alloc([128, 4, 512], BF16)
            ss = A.alloc([128, 4], F32)
            den2 = A.alloc([128, 8], F32)
            omix = A.alloc([128, 1024], BF16)
            omixT = A.alloc([128, 8, 512], BF16)

        P.dma("sp", ident[:], ident_in, writes=[("ident",)])
        for i in range(5):
            P.dma("sp", gvec[:, i, :], gvec_in[i], writes=[("gvec",)])
        P.op("dve", lambda e: e.memset(ones_bf[:], 1.0 / D), writes=[("ones",)])
        P.op("dve", lambda e: e.memset(ones1[:], 1.0), writes=[("ones1",)])
        if FFN:
            P.dma("sp", convp[:], convp_in, writes=[("convp",)])
            P.op("dve", lambda e: e.memset(halo[:], 0.0), writes=[("halo",)])
        if MIX:
            P.dma("pool", w_in[:, :, 0:512], w_in_v[:, :, 0:512], writes=[("w_in", 0)])
            P.dma("pool", w_in[:, :, 512:640], w_in_v[:, :, 2192:2320], writes=[("w_in", 1)])
            P.dma("pool", w_in[:, :, 640:2192], w_in_v[:, :, 512:2064], writes=[("w_in", 2)])
            for kvh in range(2):
                for d_ in range(2):
                    P.dma("pool", skdup[:, :, kvh * 128 + d_ * 64: kvh * 128 + d_ * 64 + 64],
                          w_in_v[:, :, 2064 + kvh * 64: 2064 + kvh * 64 + 64],
                          writes=[("skdup", kvh, d_)])
            P.dma("pool", w_out[:], w_out_v, writes=[("w_out",)])
            P.dma("pool", identb[:], ident_in, writes=[("identb",)])
            P.dma("pool", walpha[0:32, :], walpha_in, writes=[("walpha",)])
            P.dma("pool", maskA[:], maskA_in, writes=[("maskA",)])
            P.dma("sp", gain[:], gain_in, writes=[("gain",)])
            P.dma("sp", triLE[:], tri_in[0], writes=[("triLE",)])
            P.dma("sp", triGT[:], tri_in[1], writes=[("triGT",)])
            P.dma("sp", bias_f[:], bias_in, writes=[("@s.bias",)])
            P.dma("sp", negm[:], negm_in, writes=[("@s.negm",)])
            P.dma("sp", sinks_b[:], sinks_in, writes=[("@s.sinks",)])
            for kvh in range(2):
                P.op("dve", lambda e, kvh=kvh: e.tensor_tensor(
                    out=biasm[:, kvh, :, :], in0=bias_f[:, kvh, :, :], in1=negm[:, :, :],
                    op=ALU.add), reads=[("@s.bias",), ("@s.negm",)], writes=[("biasm",)])
            P.op("act", lambda e: e.activation(out=esink[:], in_=sinks_b[:], func=AF.Exp),
                 reads=[("@s.sinks",)], writes=[("esink",)])
            P.op("dve", lambda e: e.memset(S[:], 0.0), writes=[("S",)])
            P.op("dve", lambda e: e.memset(Sbf[:], 0.0), writes=[("Sbf",)])
            P.op("dve", lambda e: e.memset(skT[:], 0.0), writes=[("skT", 0), ("skT", 1)])
            P.op("dve", lambda e: e.memset(vext[:], 1.0), writes=[("vext", 0), ("vext", 1)])
            P.fence()

        W_IN_KEYS = [("w_in", 0), ("w_in", 1), ("w_in", 2)]

        def rms_stats(src_fn, src_keys, ncols, sq, sq_key, out_rstd, out_key):
            b = P.bank()
            for c in range(8):
                s = c % 2
                P.op("act", lambda e, c=c, s=s: e.activation(
                    out=sq[:, s, :ncols], in_=src_fn(c), func=AF.Square),
                    reads=[src_keys[c]], writes=[(sq_key, s)])
                P.op("pe", lambda e, c=c, s=s, b=b: e.matmul(
                    banks[b][:, :ncols], lhsT=ones_bf[:], rhs=sq[:, s, :ncols],
                    start=(c == 0), stop=(c == 7)),
                    reads=[(sq_key, s), ("ones",)], writes=[("ps", b)])
            P.op("act", lambda e, b=b: e.activation(
                out=out_rstd[:, :ncols], in_=banks[b][:, :ncols], func=AF.Ln, bias=float(EPS)),
                reads=[("ps", b)], writes=[out_key])
            P.op("act", lambda e: e.activation(
                out=out_rstd[:, :ncols], in_=out_rstd[:, :ncols], func=AF.Exp, scale=-0.5),
                reads=[out_key], writes=[out_key])

        def hslice(c, g):
            return hT[:, c, g * 512:(g + 1) * 512]

        def hkeys(g):
            return [("hT", c, g) for c in range(8)]

        def evac(i, out_ap, bank_ap, reads, writes, scale=None):
            if i % 2 == 0:
                if scale is None:
                    P.op("act", lambda e: e.copy(out=out_ap, in_=bank_ap), reads=reads,
                         writes=writes)
                else:
                    P.op("act", lambda e: e.mul(out_ap, bank_ap, float(scale)),
                         reads=reads, writes=writes)
            else:
                if scale is None:
                    P.op("dve", lambda e: e.tensor_copy(out=out_ap, in_=bank_ap),
                         reads=reads, writes=writes)
                else:
                    P.op("dve", lambda e: e.tensor_scalar_mul(out_ap, bank_ap, float(scale)),
                         reads=reads, writes=writes)

        def norm_apply(out_fn, out_key_fn, g, gi, rstd_t, rstd_key):
            for c in range(8):
                P.op("dve", lambda e, c=c: e.scalar_tensor_tensor(
                    out=out_fn(c), in0=hslice(c, g), scalar=gvec[:, gi, c:c + 1],
                    in1=rstd_t[:, :], op0=ALU.mult, op1=ALU.mult),
                    reads=[("hT", c, g), rstd_key, ("gvec",)], writes=[out_key_fn(c)])

        def proj_block(lhs_fn, lhs_keys, rhs_fn, rhs_key_fn, out_fn, b):
            for kc in range(8):
                P.op("pe", lambda e, kc=kc: e.matmul(
                    out_fn(), lhsT=lhs_fn(kc), rhs=rhs_fn(kc), start=(kc == 0), stop=(kc == 7)),
                    reads=list(lhs_keys) + [rhs_key_fn(kc)], writes=[("ps", b)])

        def resid_add(b, fo, g):
            P.op("dve", lambda e: e.tensor_tensor(
                out=hslice(fo, g), in0=hslice(fo, g), in1=banks[b][:, :], op=ALU.add),
                reads=[("ps", b), ("hT", fo, g)], writes=[("hT", fo, g)])

        def load_quarter(q):
            for g in range(NQ_TOK // 512):
                tok0 = q * NQ_TOK + g * 512
                for t in range(4):
                    P.dma("sp", xtok[:, t, :], x[tok0 + t * 128: tok0 + (t + 1) * 128, :],
                          writes=[("@l.xtok", t)])
                for c in range(8):
                    b = P.bank()
                    for t in range(4):
                        P.op("pe", lambda e, c=c, t=t, b=b: e.transpose(
                            banks[b][:, t * 128:(t + 1) * 128], xtok[:, t, c * 128:(c + 1) * 128],
                            ident[:]),
                            reads=[("@l.xtok", t), ("ident",)], writes=[("ps", b)])
                    evac(c, hslice(c, g), banks[b][:, :], [("ps", b)], [("hT", c, g)])

        def mix_tile_A(G, g, t, n, par, tsl, hn_keys, lhs_tok, ctx):
            bK = P.bank()
            proj_block(lhs_tok, W_IN_KEYS, lambda kc: w_in[:, kc, 256:640], hn_keys,
                       lambda: banks[bK][:, 0:384], bK)
            bV = P.bank()
            proj_block(lhs_tok, W_IN_KEYS, lambda kc: w_in[:, kc, 640:1152], hn_keys,
                       lambda: banks[bV][:, :], bV)
            bR = P.bank()
            proj_block(lhs_tok, W_IN_KEYS, lambda kc: w_in[:, kc, 1152:1664], hn_keys,
                       lambda: banks[bR][:, :], bR)
            P.op("dve", lambda e: e.tensor_tensor(
                out=kte[:, :], in0=banks[bK][:, 0:256], in1=Eend[:, t * 256:(t + 1) * 256],
                op=ALU.mult), reads=[("ps", bK), ("@m.L", t // 2)], writes=[("@m.kte",)])
            P.op("act", lambda e: e.copy(out=vtok[:, :], in_=banks[bV][:, :]),
                 reads=[("ps", bV)], writes=[("@m.v",)])
            P.op("dve", lambda e: e.tensor_copy(
                out=vext[:, par, :, 0:64],
                in_=banks[bK][:, 256:384].rearrange("p (a d) -> p a d", a=2)),
                reads=[("ps", bK)], writes=[("vext", par)])
            P.op("act", lambda e: e.activation(out=er[:, :], in_=banks[bR][:, :], func=AF.Exp,
                                               scale=-1.0),
                 reads=[("ps", bR)], writes=[("@m.sq", 0), ("@m.sq", 1)])
            P.op("dve", lambda e: e.tensor_scalar_add(er[:, :], er[:, :], 1.0),
                 reads=[("@m.sq", 0), ("@m.sq", 1)], writes=[("@m.sq", 0), ("@m.sq", 1)])
            P.op("dve", lambda e: e.reciprocal(out=er[:, :], in_=er[:, :]),
                 reads=[("@m.sq", 0), ("@m.sq", 1)], writes=[("@m.sq", 0), ("@m.sq", 1)])
            P.op("dve", lambda e: e.tensor_tensor(out=er[:, :], in0=banks[bR][:, :],
                                                  in1=er[:, :], op=ALU.mult),
                 reads=[("ps", bR), ("@m.sq", 0), ("@m.sq", 1)], writes=[("@m.sq", 0), ("@m.sq", 1)])
            kts = [1] if n == 0 else [0, 1]
            for kvh in range(2):
                for kt in kts:
                    koff = 128 + t * 128 if kt == 1 else t * 128
                    for pp in range(2):
                        bs = P.bank()
                        P.op("pe", lambda e, bs=bs, koff=koff, kvh=kvh, pp=pp: e.matmul(
                            banks[bs][:, 0:256],
                            lhsT=skT[pp * 64:(pp + 1) * 64, kvh, koff:koff + 128],
                            rhs=sqT[pp * 64:(pp + 1) * 64, 2 * kvh:2 * kvh + 2, tsl],
                            start=True, stop=False),
                            reads=[("skT", kvh), ("@m.sqT", 2 * kvh), ("@m.sqT", 2 * kvh + 1)],
                            writes=[("ps", bs)])
                        P.op("pe", lambda e, bs=bs, kvh=kvh, kt=kt, pp=pp: e.matmul(
                            banks[bs][:, 0:256], lhsT=identb[:, :],
                            rhs=biasm[:, kvh, kt, pp * 256:(pp + 1) * 256],
                            start=False, stop=True),
                            reads=[("identb",), ("biasm",)], writes=[("ps", bs)])
                        P.op("act", lambda e, bs=bs, kvh=kvh, kt=kt, pp=pp: e.activation(
                            out=pTm[:, kvh * 2 + kt, :].rearrange(
                                "p (bb pp q) -> p pp bb q", bb=2, pp=2)[:, pp],
                            in_=banks[bs][:, 0:256].rearrange("p (bb q) -> p bb q", bb=2),
                            func=AF.Exp),
                            reads=[("ps", bs)], writes=[("@m.pT", kvh * 2 + kt)])
            bA2 = [P.bank(), P.bank()]
            for h in range(4):
                hp, hb = h % 2, h // 2
                P.op("pe", lambda e, h=h, hp=hp, hb=hb: e.matmul(
                    banks[bA2[hp]][:, hb * 128:(hb + 1) * 128],
                    lhsT=kinv[hp * 64:(hp + 1) * 64, hb, tsl],
                    rhs=qdec[hp * 64:(hp + 1) * 64, hb, tsl], start=True, stop=True),
                    reads=[("@m.kinv", hb), ("@m.qdec", hb)], writes=[("ps", bA2[hp])])
            for hp in range(2):
                P.op("dve", lambda e, hp=hp: e.tensor_tensor(
                    out=attTm[:, :].rearrange("p (hb hp i) -> p hp hb i", hb=2, hp=2)[:, hp],
                    in0=banks[bA2[hp]][:, 0:256].rearrange("p (hb i) -> p hb i", hb=2),
                    in1=maskA[:, :].unsqueeze(1).to_broadcast([128, 2, 128]), op=ALU.mult),
                    reads=[("ps", bA2[hp]), ("maskA",)], writes=[("@m.att",)])
            ctx["bR"] = bR
            ctx["bA2"] = bA2

        def mix_tile_B(G, g, t, n, par, tsl, ctx):
            bR = ctx["bR"]
            bA2 = ctx["bA2"]
            bKV = P.bank()
            for h in range(4):
                hp, hb = h % 2, h // 2
                P.op("pe", lambda e, h=h, hp=hp, hb=hb: e.matmul(
                    banks[bKV][hp * 64:(hp + 1) * 64, hb * 128:(hb + 1) * 128],
                    lhsT=kte[:, h * 64:(h + 1) * 64], rhs=vtok[:, h * 128:(h + 1) * 128],
                    start=True, stop=True),
                    reads=[("@m.kte",), ("@m.v",)], writes=[("ps", bKV)])
            bO2 = [P.bank(), P.bank()]
            for h in range(4):
                hp, hb = h % 2, h // 2
                P.op("pe", lambda e, h=h, hp=hp, hb=hb: e.matmul(
                    banks[bO2[hp]][:, hb * 128:(hb + 1) * 128],
                    lhsT=attTm[:, h * 128:(h + 1) * 128],
                    rhs=vtok[:, h * 128:(h + 1) * 128], start=True, stop=False),
                    reads=[("@m.att",), ("@m.v",)], writes=[("ps", bO2[hp])])
                P.op("pe", lambda e, h=h, hp=hp, hb=hb: e.matmul(
                    banks[bO2[hp]][:, hb * 128:(hb + 1) * 128],
                    lhsT=qdec[hp * 64:(hp + 1) * 64, hb, tsl],
                    rhs=Sbf[hp * 64:(hp + 1) * 64, hb, :], start=False, stop=True),
                    reads=[("@m.qdec", hb), ("Sbf",)], writes=[("ps", bO2[hp])])
            for hb in range(2):
                P.op("dve", lambda e, hb=hb: e.scalar_tensor_tensor(
                    out=S[:, hb, :], in0=S[:, hb, :],
                    scalar=Ebc[:, hb, t * 128 + 127:t * 128 + 128],
                    in1=banks[bKV][:, hb * 128:(hb + 1) * 128], op0=ALU.mult, op1=ALU.add),
                    reads=[("S",), ("@m.Ebc", hb), ("ps", bKV)], writes=[("S",)])
            P.op("act", lambda e: e.copy(out=Sbf[:], in_=S[:]), reads=[("S",)],
                 writes=[("Sbf",)])
            P.op("dve", lambda e: e.memset(ss[:, :], 0.0), writes=[("@m.ss",)])
            for h in range(4):
                P.op("act", lambda e, h=h: e.activation(
                    out=junk[:, :], in_=banks[bO2[h % 2]][:, (h // 2) * 128:(h // 2 + 1) * 128],
                    func=AF.Square, accum_out=ss[:, h:h + 1]),
                    reads=[("ps", bO2[h % 2]), ("@m.ss",)], writes=[("@m.ss",), ("@m.rstd",)])
            P.op("act", lambda e: e.activation(out=ss[:, :], in_=ss[:, :], func=AF.Ln,
                                               bias=float(EPS), scale=1.0 / 128.0),
                 reads=[("@m.ss",)], writes=[("@m.ss",)])
            P.op("act", lambda e: e.activation(out=ss[:, :], in_=ss[:, :], func=AF.Exp,
                                               scale=-0.5),
                 reads=[("@m.ss",)], writes=[("@m.ss",)])
            P.op("dve", lambda e: e.tensor_tensor(
                out=er[:, :].rearrange("p (h v) -> p h v", h=4),
                in0=er[:, :].rearrange("p (h v) -> p h v", h=4),
                in1=ss[:, :].unsqueeze(2).to_broadcast([128, 4, 128]), op=ALU.mult),
                reads=[("@m.sq", 0), ("@m.sq", 1), ("@m.ss",)], writes=[("@m.sq", 0), ("@m.sq", 1)])
            for hp in range(2):
                P.op("dve", lambda e, hp=hp: e.tensor_tensor(
                    out=omix[:, 0:512].rearrange("p (hb hp v) -> p hp hb v", hb=2, hp=2)[:, hp],
                    in0=banks[bO2[hp]][:, 0:256].rearrange("p (hb v) -> p hb v", hb=2),
                    in1=er[:, :].rearrange("p (hb hp v) -> p hp hb v", hb=2, hp=2)[:, hp],
                    op=ALU.mult),
                    reads=[("ps", bO2[hp]), ("@m.sq", 0), ("@m.sq", 1)], writes=[("@m.omix", 0)])
            kts = [1] if n == 0 else [0, 1]
            for kvh in range(2):
                bo = P.bank()
                for j in range(4):
                    for idx, kt in enumerate(kts):
                        pk = par if kt == 1 else 1 - par
                        P.op("pe", lambda e, bo=bo, j=j, kt=kt, pk=pk, idx=idx, kvh=kvh: e.matmul(
                            banks[bo][:, j * 65:(j + 1) * 65],
                            lhsT=pTm[:, kvh * 2 + kt, j * 128:(j + 1) * 128],
                            rhs=vext[:, pk, kvh, 0:65], start=(idx == 0),
                            stop=(idx == len(kts) - 1)),
                            reads=[("@m.pT", kvh * 2 + kt), ("vext", pk)], writes=[("ps", bo)])
                ov = banks[bo][:, 0:260].rearrange("p (j e) -> p j e", e=65)
                P.op("dve", lambda e, ov=ov, kvh=kvh: e.tensor_tensor(
                    out=den2[:, kvh * 4:(kvh + 1) * 4], in0=ov[:, :, 64],
                    in1=esink[:, kvh * 4:(kvh + 1) * 4], op=ALU.add),
                    reads=[("ps", bo), ("esink",)], writes=[("@m.den", kvh)])
                P.op("dve", lambda e, kvh=kvh: e.reciprocal(
                    out=den2[:, kvh * 4:(kvh + 1) * 4], in_=den2[:, kvh * 4:(kvh + 1) * 4]),
                    reads=[("@m.den", kvh)], writes=[("@m.den", kvh)])
                P.op("dve", lambda e, ov=ov, kvh=kvh: e.tensor_tensor(
                    out=omix[:, 512 + kvh * 256: 512 + (kvh + 1) * 256].rearrange(
                        "p (j d) -> p j d", d=64),
                    in0=ov[:, :, 0:64],
                    in1=den2[:, kvh * 4:(kvh + 1) * 4].unsqueeze(2).to_broadcast([128, 4, 64]),
                    op=ALU.mult),
                    reads=[("ps", bo), ("@m.den", kvh)], writes=[("@m.omix", 1 + kvh)])

        def mix_tile_C(G, g, t, n, par, tsl, ctx):
            bT2 = [P.bank(), P.bank()]
            for c in range(8):
                P.op("pe", lambda e, c=c: e.matmul(
                    banks[bT2[c // 4]][:, (c % 4) * 128:(c % 4 + 1) * 128],
                    lhsT=omix[:, c * 128:(c + 1) * 128], rhs=identb[:, :], start=True, stop=True),
                    reads=[("@m.omix", 0 if c < 4 else (1 if c < 6 else 2)), ("identb",)],
                    writes=[("ps", bT2[c // 4])])
            P.op("act", lambda e: e.activation(
                out=omixT[:, 0:4, tsl],
                in_=banks[bT2[0]][:, :].rearrange("p (c k) -> p c k", c=4),
                func=AF.Identity, scale=gain[:, 0:1]),
                reads=[("ps", bT2[0]), ("gain",)], writes=[("@m.omixT", 0)])
            P.op("dve", lambda e: e.tensor_copy(
                out=omixT[:, 4:8, tsl],
                in_=banks[bT2[1]][:, :].rearrange("p (c k) -> p c k", c=4)),
                reads=[("ps", bT2[1])], writes=[("@m.omixT", 1)])

        def mix_tile(G, g, t, st, ctx):
            n = G * 4 + t
            par = n % 2
            tsl = slice(t * 128, (t + 1) * 128)
            hn_keys = lambda kc: ("@m.hn", kc)
            lhs_tok = lambda kc: hnT[:, kc, tsl]
            if st == "A":
                mix_tile_A(G, g, t, n, par, tsl, hn_keys, lhs_tok, ctx)
            elif st == "B":
                mix_tile_B(G, g, t, n, par, tsl, ctx)
            else:
                mix_tile_C(G, g, t, n, par, tsl, ctx)

        def mix_group(q, g):
            G = q * 2 + g
            rms_stats(lambda c: hslice(c, g), hkeys(g), 512, m_sq, "@m.sq", m_rstd, ("@m.rstd",))
            norm_apply(lambda c: hnT[:, c, :], lambda c: ("@m.hn", c), g, G_MIX, m_rstd,
                       ("@m.rstd",))
            hn_keys = lambda kc: ("@m.hn", kc)
            rhs_hn = lambda kc: hnT[:, kc, :]
            P.op("dve", lambda e: e.memset(alow[0:32, :], 1.0), writes=[("@m.rstd",)])
            b = P.bank()
            proj_block(lambda kc: w_in[:, kc, 1664:1680], W_IN_KEYS, rhs_hn, hn_keys,
                       lambda b=b: banks[b][0:16, :], b)
            P.op("act", lambda e, b=b: e.copy(out=alow[0:16, :], in_=banks[b][0:16, :]),
                 reads=[("ps", b)], writes=[("@m.rstd",)])
            for b4 in range(4):
                b = P.bank()
                proj_block(lambda kc, b4=b4: w_in[:, kc, 1680 + b4 * 128:1680 + (b4 + 1) * 128],
                           W_IN_KEYS, rhs_hn, hn_keys, lambda b=b: banks[b][:, :], b)
                evac(b4, sqT[:, b4, :], banks[b][:, :], [("ps", b)], [("@m.sqT", b4)],
                     scale=0.125)
            for kvh in range(2):
                b = P.bank()
                proj_block(lambda kc, kvh=kvh: skdup[:, kc, kvh * 128:(kvh + 1) * 128],
                           [("skdup", kvh, 0), ("skdup", kvh, 1)], rhs_hn, hn_keys,
                           lambda b=b: banks[b][:, :], b)
                evac(kvh, skT[:, kvh, 128:640], banks[b][:, :], [("ps", b)], [("skT", kvh)])
            if MIX_STOP <= 1:
                return
            zb = [P.bank(), P.bank()]
            for t in range(4):
                P.op("pe", lambda e, t=t: e.matmul(
                    banks[zb[t // 2]][:, (t % 2) * 256:(t % 2 + 1) * 256],
                    lhsT=alow[0:32, t * 128:(t + 1) * 128], rhs=walpha[0:32, :],
                    start=True, stop=True),
                    reads=[("@m.rstd",), ("walpha",)], writes=[("ps", zb[t // 2])])
            for i in range(2):
                P.op("act", lambda e, i=i: e.activation(
                    out=L[:, i * 512:(i + 1) * 512], in_=banks[zb[i]][:, :], func=AF.Exp,
                    scale=-1.0), reads=[("ps", zb[i])], writes=[("@m.L", i)])
                P.op("act", lambda e, i=i: e.activation(
                    out=L[:, i * 512:(i + 1) * 512], in_=L[:, i * 512:(i + 1) * 512],
                    func=AF.Ln, bias=1.0), reads=[("@m.L", i)], writes=[("@m.L", i)])
            cb = [P.bank(), P.bank()]
            for hb in range(2):
                for t in range(4):
                    P.op("pe", lambda e, hb=hb, t=t: e.matmul(
                        banks[cb[hb]][:, t * 128:(t + 1) * 128],
                        lhsT=L[:, t * 256 + hb * 128: t * 256 + (hb + 1) * 128],
                        rhs=triLE[:, :], start=True, stop=True),
                        reads=[("@m.L", t // 2), ("triLE",)], writes=[("ps", cb[hb])])
            te = [P.bank(), P.bank()]
            for t in range(4):
                P.op("pe", lambda e, t=t: e.matmul(
                    banks[te[t // 2]][:, (t % 2) * 256:(t % 2 + 1) * 256],
                    lhsT=triGT[:, :], rhs=L[:, t * 256:(t + 1) * 256], start=True, stop=True),
                    reads=[("@m.L", t // 2), ("triGT",)], writes=[("ps", te[t // 2])])
            for hb in range(2):
                P.op("act", lambda e, hb=hb: e.activation(
                    out=Ebc[:, hb, :], in_=banks[cb[hb]][:, :], func=AF.Exp),
                    reads=[("ps", cb[hb])], writes=[("@m.Ebc", hb)])
                P.op("act", lambda e, hb=hb: e.activation(
                    out=Einv[:, hb, :], in_=banks[cb[hb]][:, :], func=AF.Exp, scale=-1.0),
                    reads=[("ps", cb[hb])], writes=[("@m.Einv", hb)])
            for i in range(2):
                P.op("act", lambda e, i=i: e.activation(
                    out=Eend[:, i * 512:(i + 1) * 512], in_=banks[te[i]][:, :], func=AF.Exp),
                    reads=[("ps", te[i])], writes=[("@m.L", i)])
            for hb in range(2):
                b = P.bank()
                proj_block(lambda kc, hb=hb: w_in[:, kc, hb * 128:(hb + 1) * 128], W_IN_KEYS,
                           rhs_hn, hn_keys, lambda b=b: banks[b][:, :], b)
                P.op("dve", lambda e, b=b, hb=hb: e.scalar_tensor_tensor(
                    out=qdec[:, hb, :], in0=banks[b][:, :], scalar=0.125, in1=Ebc[:, hb, :],
                    op0=ALU.mult, op1=ALU.mult),
                    reads=[("ps", b), ("@m.Ebc", hb)], writes=[("@m.qdec", hb)])
            for hb in range(2):
                b = P.bank()
                proj_block(lambda kc, hb=hb: w_in[:, kc, 256 + hb * 128:256 + (hb + 1) * 128],
                           W_IN_KEYS, rhs_hn, hn_keys, lambda b=b: banks[b][:, :], b)
                P.op("dve", lambda e, b=b, hb=hb: e.tensor_tensor(
                    out=kinv[:, hb, :], in0=banks[b][:, :], in1=Einv[:, hb, :], op=ALU.mult),
                    reads=[("ps", b), ("@m.Einv", hb)], writes=[("@m.kinv", hb)])
            if MIX_STOP <= 2:
                return
            ctxs = [dict() for _ in range(4)]
            for t in range(4):
                mix_tile(G, g, t, "A", ctxs[t])
                if t > 0:
                    mix_tile(G, g, t - 1, "C", ctxs[t - 1])
                mix_tile(G, g, t, "B", ctxs[t])
            mix_tile(G, g, 3, "C", ctxs[3])
            if MIX_STOP <= 7:
                return
            for fo in range(8):
                b = P.bank()
                proj_block(lambda kc, fo=fo: w_out[:, kc, fo * 128:(fo + 1) * 128], [("w_out",)],
                           lambda kc: omixT[:, kc, :], lambda kc: ("@m.omixT", kc // 4),
                           lambda b=b: banks[b][:, :], b)
                resid_add(b, fo, g)
            P.op("dve", lambda e: e.tensor_copy(out=skT[:, :, 0:128], in_=skT[:, :, 512:640]),
                 reads=[("skT", 0), ("skT", 1)], writes=[("skT", 0), ("skT", 1)])

        def mem_kv():
            for t in range(2):
                P.dma("sp", k_xtok[:, t, :], mem[t * 128:(t + 1) * 128, :],
                      writes=[("@k.xtok", t)])
            P.dma("pool", wk[:], wk_v, writes=[("@k.wk",)])
            for c in range(8):
                b = P.bank()
                for t in range(2):
                    P.op("pe", lambda e, c=c, t=t, b=b: e.transpose(
                        banks[b][:, t * 128:(t + 1) * 128], k_xtok[:, t, c * 128:(c + 1) * 128],
                        ident[:]),
                        reads=[("@k.xtok", t), ("ident",)], writes=[("ps", b)])
                evac(c, memT[:, c, :], banks[b][:, 0:256], [("ps", b)], [("@k.memT", c)])
            rms_stats(lambda c: memT[:, c, :], [("@k.memT", c) for c in range(8)], 256,
                      k_sq, "@k.sq", k_rstd, ("@k.rstd",))
            for c in range(8):
                P.op("dve", lambda e, c=c: e.scalar_tensor_tensor(
                    out=memn[:, c, :], in0=memT[:, c, :], scalar=gvec[:, G_MEM, c:c + 1],
                    in1=k_rstd[:, 0:256], op0=ALU.mult, op1=ALU.mult),
                    reads=[("@k.memT", c), ("@k.rstd",), ("gvec",)], writes=[("@k.memn", c)])
            for dc in range(8):
                b = P.bank()
                proj_block(lambda kc, dc=dc: wk[:, kc, dc * 128:(dc + 1) * 128], [("@k.wk",)],
                           lambda kc: memn[:, kc, :], lambda kc: ("@k.memn", kc),
                           lambda b=b: banks[b][:, 0:256], b)
                evac(dc, KT[:, dc, :], banks[b][:, 0:256], [("ps", b)], [("KT",)])
            P.dma("pool", wv[:], wv_v, writes=[("@k.wk",)])
            for mt in range(2):
                for half in range(2):
                    b = P.bank()
                    proj_block(lambda kc, mt=mt: memn[:, kc, mt * 128:(mt + 1) * 128],
                               [("@k.wk",)],
                               lambda kc, half=half: wv[:, kc, half * 512:(half + 1) * 512],
                               lambda kc: ("@k.memn", kc), lambda b=b: banks[b][:, :], b)
                    evac(half, Vm[:, mt, half * 512:(half + 1) * 512], banks[b][:, :],
                         [("ps", b)], [("Vm",)])

        def cross_head(h, hp):
            for mt in range(2):
                b = P.bank()
                for d2 in range(2):
                    dc = 2 * h + d2
                    P.op("pe", lambda e, b=b, dc=dc, d2=d2, mt=mt: e.matmul(
                        banks[b][:, :], lhsT=KT[:, dc, mt * 128:(mt + 1) * 128], rhs=cq[:, dc, :],
                        start=(d2 == 0), stop=(d2 == 1)),
                        reads=[("KT",), ("@b.cq", dc)], writes=[("ps", b)])
                P.op("act", lambda e, b=b, mt=mt: e.activation(
                    out=pT[:, hp, mt, :], in_=banks[b][:, :], func=AF.Exp),
                    reads=[("ps", b)], writes=[("@b.pT", hp, mt)])
            b = P.bank()
            for mt in range(2):
                P.op("pe", lambda e, b=b, mt=mt: e.matmul(
                    banks[b][:, :], lhsT=ones1[:], rhs=pT[:, hp, mt, :],
                    start=(mt == 0), stop=(mt == 1)),
                    reads=[("ones1",), ("@b.pT", hp, mt)], writes=[("ps", b)])
            P.op("dve", lambda e, b=b: e.reciprocal(out=rden[:, hp, :], in_=banks[b][:, :]),
                 reads=[("ps", b)], writes=[("@b.rden", hp)])
            for d2 in range(2):
                dc = 2 * h + d2
                b = P.bank()
                for mt in range(2):
                    P.op("pe", lambda e, b=b, dc=dc, mt=mt: e.matmul(
                        banks[b][:, :], lhsT=Vm[:, mt, dc * 128:(dc + 1) * 128],
                        rhs=pT[:, hp, mt, :], start=(mt == 0), stop=(mt == 1)),
                        reads=[("Vm",), ("@b.pT", hp, mt)], writes=[("ps", b)])
                P.op("dve", lambda e, b=b, dc=dc: e.tensor_tensor(
                    out=onT[:, dc, :], in0=banks[b][:, :], in1=rden[:, hp, :], op=ALU.mult),
                    reads=[("ps", b), ("@b.rden", hp)], writes=[("@b.onT", dc)])

        def cross(q):
            for g in range(2):
                rms_stats(lambda c, g=g: hslice(c, g), hkeys(g), 512, b_sq, "@b.sq", b_rstd,
                          ("@b.rstd",))
                norm_apply(lambda c: hn2[:, c, :], lambda c: ("@b.hn2", c), g, G_CROSS, b_rstd,
                           ("@b.rstd",))
                for dc in range(8):
                    b = P.bank()
                    proj_block(lambda kc, dc=dc: wq[:, kc, dc * 128:(dc + 1) * 128], [("wq",)],
                               lambda kc: hn2[:, kc, :], lambda kc: ("@b.hn2", kc),
                               lambda b=b: banks[b][:, :], b)
                    evac(dc, cq[:, dc, :], banks[b][:, :], [("ps", b)], [("@b.cq", dc)],
                         scale=1.0 / 16.0)
                for h in range(4):
                    cross_head(h, h % 2)
                for fo in range(8):
                    b = P.bank()
                    proj_block(lambda kc, fo=fo: wo[:, kc, fo * 128:(fo + 1) * 128], [("wo",)],
                               lambda kc: onT[:, kc, :], lambda kc: ("@b.onT", kc),
                               lambda b=b: banks[b][:, :], b)
                    resid_add(b, fo, g)

        def ffn_load_wup(j):
            slot = j % 3
            for gv in range(2):
                P.dma("pool", wup[:, slot, :, gv, :], w_up_v[:, :, gv, j, :],
                      writes=[("wup", slot, gv)])

        def ffn_load_wdn(w):
            a, bnd = WAVES[w]
            P.dma("pool", wdn[:, 0:bnd - a, :], w_down_v[:, a:bnd, :], writes=[("wdn",)])

        def ffn_prefetch(q):
            ffn_load_wup(0)
            ffn_load_wup(1)

        ffn_cnt = [0]

        def ffn_sub(j, jl, wbuf, slot, e0, n, ts):
            bks = []
            for gv in range(2):
                b = P.bank()
                bks.append(b)
                for kc in range(8):
                    P.op("pe", lambda e, b=b, kc=kc, gv=gv: e.matmul(
                        banks[b][:, :n + 2], lhsT=wup[:, slot, kc, gv, :],
                        rhs=hn3[:, kc, e0:e0 + n + 2], start=(kc == 0), stop=(kc == 7)),
                        reads=[("wup", slot, gv), ("@c.hn3", kc)], writes=[("ps", b)])
            for gv in range(2):
                jj = j + 22 * gv
                b = bks[gv]
                P.op("act", lambda e, b=b, jj=jj, gv=gv: e.activation(
                    out=tA[:, ts, gv, :n], in_=banks[b][:, 2:n + 2], func=AF.Identity,
                    bias=convp[:, jj, 3:4], scale=convp[:, jj, 2:3]),
                    reads=[("ps", b), ("convp",)], writes=[("@c.tA", ts, gv)])
                P.op("dve", lambda e, b=b, jj=jj, gv=gv: e.scalar_tensor_tensor(
                    out=tB[:, ts, gv, :n], in0=banks[b][:, 1:n + 1], scalar=convp[:, jj, 1:2],
                    in1=tA[:, ts, gv, :n], op0=ALU.mult, op1=ALU.add),
                    reads=[("ps", b), ("@c.tA", ts, gv), ("convp",)], writes=[("@c.tB", ts, gv)])
                P.op("dve", lambda e, b=b, jj=jj, gv=gv: e.scalar_tensor_tensor(
                    out=tA[:, ts, gv, :n], in0=banks[b][:, 0:n], scalar=convp[:, jj, 0:1],
                    in1=tB[:, ts, gv, :n], op0=ALU.mult, op1=ALU.add),
                    reads=[("ps", b), ("@c.tB", ts, gv), ("convp",)], writes=[("@c.tA", ts, gv)])
            P.op("act", lambda e: e.activation(
                out=tB[:, ts, 0, :n], in_=tA[:, ts, 0, :n], func=AF.Silu),
                reads=[("@c.tA", ts, 0)], writes=[("@c.tB", ts, 0)])
            P.op("dve", lambda e: e.tensor_tensor(
                out=actT[:, wbuf, jl, e0:e0 + n], in0=tB[:, ts, 0, :n], in1=tA[:, ts, 1, :n],
                op=ALU.mult),
                reads=[("@c.tB", ts, 0), ("@c.tA", ts, 1)], writes=[("@c.actT", wbuf, jl)])

        def ffn_pair(j, jl, wbuf):
            for (e0, n) in SUBS:
                ts = ffn_cnt[0] % 2
                ffn_cnt[0] += 1
                ffn_sub(j, jl, wbuf, j % 3, e0, n, ts)

        def ffn_down(w):
            a, bnd = WAVES[w]
            wbuf = w % 2
            cnt = bnd - a
            for fo in range(8):
                for s2 in range(2):
                    b = P.bank()
                    for jl in range(cnt):
                        P.op("pe", lambda e, b=b, jl=jl, fo=fo, s2=s2: e.matmul(
                            banks[b][:, :], lhsT=wdn[:, jl, fo * 128:(fo + 1) * 128],
                            rhs=actT[:, wbuf, jl, s2 * 512:(s2 + 1) * 512],
                            start=(jl == 0), stop=(jl == cnt - 1)),
                            reads=[("wdn",), ("@c.actT", wbuf, jl)], writes=[("ps", b)])
                    resid_add(b, fo, s2)

        def ffn(q):
            hn3_keys = [("@c.hn3", c) for c in range(8)]
            P.op("dve", lambda e: e.tensor_copy(out=hn3[:, :, 0:2], in_=halo[:, :, :]),
                 reads=[("halo",)], writes=hn3_keys)
            for g in range(2):
                rms_stats(lambda c, g=g: hslice(c, g), hkeys(g), 512, c_sq, "@c.sq", c_rstd,
                          ("@c.rstd",))
                norm_apply(lambda c, g=g: hn3[:, c, 2 + g * 512: 2 + (g + 1) * 512],
                           lambda c: ("@c.hn3", c), g, G_FFN, c_rstd, ("@c.rstd",))
            P.op("dve", lambda e: e.tensor_copy(out=halo[:, :, :],
                                                in_=hn3[:, :, NQ_TOK:NQ_TOK + 2]),
                 reads=hn3_keys, writes=[("halo",)])
            ffn_load_wdn(0)
            for w, (a, bnd) in enumerate(WAVES):
                for j in range(a, bnd):
                    if j + 2 < 22:
                        ffn_load_wup(j + 2)
                    ffn_pair(j, j - a, w % 2)
                    if j == a and w > 0:
                        ffn_down(w - 1)
                        ffn_load_wdn(w)
            ffn_down(len(WAVES) - 1)

        def final(q):
            for g in range(NQ_TOK // 512):
                tok0 = q * NQ_TOK + g * 512
                rms_stats(lambda c, g=g: hslice(c, g), hkeys(g), 512, f_sq, "@f.sq", f_rstd,
                          ("@f.rstd",))
                norm_apply(lambda c: yT[:, c, :], lambda c: ("@f.yT", c), g, G_FIN, f_rstd,
                           ("@f.rstd",))
                for t in range(4):
                    s = t % 2
                    for half in range(2):
                        b = P.bank()
                        for cc in range(4):
                            c = half * 4 + cc
                            P.op("pe", lambda e, c=c, cc=cc, t=t, b=b: e.transpose(
                                banks[b][:, cc * 128:(cc + 1) * 128],
                                yT[:, c, t * 128:(t + 1) * 128], ident[:]),
                                reads=[("@f.yT", c), ("ident",)], writes=[("ps", b)])
                        evac(half, ytok[:, s, half * 512:(half + 1) * 512], banks[b][:, :],
                             [("ps", b)], [("@f.ytok", s, half)])
                    P.dma("sp", y[tok0 + t * 128: tok0 + (t + 1) * 128, :], ytok[:, s, :],
                          reads=[("@f.ytok", s, 0), ("@f.ytok", s, 1)])

        if CROSS:
            mem_kv()
            P.fence()
            P.dma("pool", wq[:], wq_v, writes=[("wq",)])
            P.dma("pool", wo[:], wo_v, writes=[("wo",)])
        for q in range(NQ):
            if FFN:
                ffn_prefetch(q)
            load_quarter(q)
            P.fence()
            if MIX:
                for g in range(2):
                    mix_group(q, g)
                P.fence()
            if CROSS:
                cross(q)
                P.fence()
            if FFN:
                ffn(q)
                P.fence()
            final(q)
            P.fence()

        with nc.Block() as block:
            P.emit(sems, dsems, {"sp": block.sync, "pe": block.tensor, "act": block.scalar,
                                 "dve": block.vector, "pool": block.gpsimd})
    return nc


def _fm(v):
    return np.ascontiguousarray(np.asarray(v, np.float32).reshape(8, 128).T)


def _t5_bucket(dist):
    n = np.maximum(dist, 0)
    max_exact = 16
    nf = np.maximum(n, 1).astype(np.float32)
    large = max_exact + (np.log(nf / max_exact) / math.log(128 / max_exact)
                         * (32 - max_exact)).astype(np.int32)
    large = np.minimum(large, 31)
    return np.where(n < max_exact, n, large)


def host_consts(inputs, phases):
    c = {}
    c["ident"] = np.eye(128, dtype=np.float32)
    z8 = np.zeros(1024, np.float32)
    gv = [inputs["norm_final"], inputs["norm_ffn"][0], inputs["norm_cross"][0],
          inputs["norm_mem"][0], inputs["norm_mix"][0]]
    c["gvec"] = np.ascontiguousarray(np.stack([_fm(v) for v in gv], axis=0))
    if "cross" in phases:
        for k in ("w_q_c", "w_k_c", "w_v_c", "w_o_c"):
            c[k] = np.ascontiguousarray(inputs[k][0], dtype=np.float32)
    if "ffn" in phases:
        cw = np.asarray(inputs["conv_w"][0], np.float32)
        cb = np.asarray(inputs["conv_b"][0], np.float32)
        cp = np.concatenate([cw, cb[None]], axis=0)
        c["convp"] = np.ascontiguousarray(cp.reshape(4, 44, 128).transpose(2, 1, 0))
        c["w_up"] = np.ascontiguousarray(inputs["w_up"][0], dtype=np.float32)
        c["w_down"] = np.ascontiguousarray(inputs["w_down"][0], dtype=np.float32)
    if "mix" in phases:
        c["w_in"] = np.ascontiguousarray(inputs["w_in"][0], dtype=np.float32)
        c["w_out"] = np.ascontiguousarray(inputs["w_out"][0], dtype=np.float32)
        wa = np.zeros((32, 256), np.float32)
        wa[0:16] = np.asarray(inputs["w_alpha"][0], np.float32)
        wa[16] = np.asarray(inputs["b_alpha"][0], np.float32)
        c["walpha"] = wa
        c["gain"] = np.ascontiguousarray(
            np.asarray(inputs["gla_gain"][0], np.float32).reshape(128, 1))
        jj, ii = np.meshgrid(np.arange(128), np.arange(128), indexing="ij")
        causal = (jj <= ii).astype(np.float32)
        c["maskA"] = np.ascontiguousarray(causal)
        tri = np.zeros((2, 128, 128), np.float32)
        tri[0] = np.where(jj <= ii, -1.0 / 16.0, 0.0)
        tri[1] = np.where(jj > ii, -1.0 / 16.0, 0.0)
        c["tri"] = tri
        rel = np.asarray(inputs["rel_bias"], np.float32)
        kk, qq = np.meshgrid(np.arange(128), np.arange(128), indexing="ij")
        bias = np.zeros((128, 2, 2, 2, 2, 128), np.float32)
        negm = np.zeros((128, 2, 4, 128), np.float32)
        for kt in range(2):
            dist = (qq + 128 - kk) if kt == 0 else (qq - kk)
            ok = (dist >= 0) & (dist < 128)
            bk = _t5_bucket(dist)
            for kvh in range(2):
                for g in range(4):
                    bias[:, kvh, kt, g % 2, g // 2, :] = rel[bk, kvh * 4 + g]
            negm[:, kt, :, :] = np.where(ok, 0.0, -30000.0)[:, None, :]
        c["swabias"] = np.ascontiguousarray(bias.reshape(128, 2, 2, 512))
        c["negmask"] = np.ascontiguousarray(negm.reshape(128, 2, 512))
        c["sinks_b"] = np.ascontiguousarray(
            np.broadcast_to(np.asarray(inputs["sinks"][0], np.float32)[None, :], (128, 8)))
    return c


_NC_CACHE = {}
PHASES = ("mix", "cross", "ffn")
DEBUG = False
LAST_RES = None


def kernel(**inputs):
    x = np.asarray(inputs["x"], dtype=np.float32)
    B, T, _ = x.shape
    phases = tuple(PHASES)
    if (T, phases) not in _NC_CACHE:
        _NC_CACHE[(T, phases)] = build(T, phases)
    nc = _NC_CACHE[(T, phases)]
    c = host_consts({k: np.asarray(v) for k, v in inputs.items()}, phases)
    in_maps = []
    for b in range(B):
        m = dict(c)
        m["x"] = np.ascontiguousarray(x[b])
        if "cross" in phases:
            m["mem"] = np.ascontiguousarray(np.asarray(inputs["mem"], np.float32)[b])
        in_maps.append(m)
    res = run_bass_kernel_spmd(nc, in_maps, core_ids=list(range(B)))
    global LAST_RES
    LAST_RES = res.results
    return np.stack([np.asarray(r["y"], dtype=np.float32) for r in res.results], axis=0)
```

```python
import math
from contextlib import ExitStack

import numpy as np
import concourse.bass as bass
import concourse.mybir as mybir
from concourse.bass_utils import run_bass_kernel_spmd

F32 = mybir.dt.float32
BF16 = mybir.dt.bfloat16
AF = mybir.ActivationFunctionType
ALU = mybir.AluOpType
AX = mybir.AxisListType

COMPUTE = ("pe", "act", "dve", "pool")
N_DMA_SEMS = 6


class Prog:
    def __init__(self, nc):
        self.nc = nc
        self.streams = {e: [] for e in ("pe", "act", "dve", "pool", "sp")}
        self.last_w = {}
        self.readers = {}
        self.dma_cnt = {"sp": 0, "pool": 0}
        self.bank_rr = 0
        self.arena_toks = []
        self.fence_deps = []
        self.prev_fence_deps = []
        self.fenced = set()

    @staticmethod
    def _is_arena(k):
        return isinstance(k[0], str) and k[0].startswith("@")

    @staticmethod
    def _compress(toks):
        best = {}
        for t in toks:
            if t[0] == "c":
                key, v = ("c", t[1]), t[2]
            else:
                key, v = ("d", t[1], t[2]), t[3]
            if v > best.get(key, -1):
                best[key] = v
        return [(k + (v,)) for k, v in best.items()]

    def fence(self):
        self.prev_fence_deps = self.fence_deps
        self.fence_deps = self._compress(self.arena_toks + self.fence_deps)
        self.arena_toks = []
        self.fenced = set()

    def _deps(self, reads, writes):
        deps = []
        for k in list(reads) + list(writes):
            if self._is_arena(k) and k not in self.fenced:
                deps.extend(self.prev_fence_deps if k[0].startswith("@l.") else self.fence_deps)
                self.fenced.add(k)
        for r in reads:
            if r in self.last_w:
                deps.append(self.last_w[r])
        for w in writes:
            if w in self.last_w:
                deps.append(self.last_w[w])
            deps.extend(self.readers.get(w, ()))
        return deps

    def _commit(self, tok, reads, writes):
        if any(self._is_arena(k) for k in list(reads) + list(writes)):
            self.arena_toks.append(tok)
            if len(self.arena_toks) > 8192:
                self.arena_toks = self._compress(self.arena_toks)
        for r in reads:
            self.readers.setdefault(r, []).append(tok)
        for w in writes:
            self.last_w[w] = tok
            self.readers[w] = []

    def op(self, eng, fn, reads=(), writes=()):
        st = self.streams[eng]
        idx = len(st)
        tok = ("c", eng, idx)
        deps = self._deps(reads, writes)
        st.append(dict(kind="op", fn=fn, deps=deps, tok=tok))
        self._commit(tok, reads, writes)
        return tok

    def dma(self, q, out, in_, reads=(), writes=(), **kw):
        st = self.streams[q]
        n = self.dma_cnt[q]
        self.dma_cnt[q] = n + 1
        semi = n % N_DMA_SEMS
        val = 16 * (n // N_DMA_SEMS + 1)
        tok = ("d", q, semi, val)
        deps = self._deps(reads, writes)
        if n >= N_DMA_SEMS:
            deps.append(("d", q, semi, val - 16))
        st.append(dict(kind="dma", out=out, in_=in_, deps=deps, tok=tok, kw=kw))
        self._commit(tok, reads, writes)
        return tok

    def bank(self):
        b = self.bank_rr
        self.bank_rr = (b + 1) % 8
        return b

    def emit(self, sems, dsems, engines):
        needed = {e: set() for e in COMPUTE}
        for e, st in self.streams.items():
            for o in st:
                for d in o["deps"]:
                    if d[0] == "c":
                        if d[1] == e and e in ("pe", "sp"):
                            continue
                        needed[d[1]].add(d[2])
        sigval = {}
        for e in COMPUTE:
            for rank, idx in enumerate(sorted(needed[e])):
                sigval[(e, idx)] = rank + 1

        def run(e, eng):
            waited = {}
            st = self.streams[e]
            for o in st:
                want = {}
                for d in o["deps"]:
                    if d[0] == "c":
                        if d[1] == e and e in ("pe", "sp"):
                            continue
                        key = ("c", d[1])
                        v = sigval[(d[1], d[2])]
                    else:
                        key = ("d", d[1], d[2])
                        v = d[3]
                    if v > waited.get(key, 0):
                        want[key] = max(want.get(key, 0), v)
                for key, v in want.items():
                    sem = sems[key[1]] if key[0] == "c" else dsems[key[1]][key[2]]
                    eng.wait_ge(sem, v)
                    waited[key] = v
                if o["kind"] == "op":
                    ins = o["fn"](eng)
                    t = o["tok"]
                    if (t[1], t[2]) in sigval:
                        ins.then_inc(sems[e], 1)
                else:
                    t = o["tok"]
                    eng.dma_start(out=o["out"], in_=o["in_"], **o["kw"]).then_inc(
                        dsems[t[1]][t[2]], 16)
            if e in ("sp", "pool"):
                n = self.dma_cnt[e]
                for semi in range(min(n, N_DMA_SEMS)):
                    cnt = (n - 1 - semi) // N_DMA_SEMS + 1
                    eng.wait_ge(dsems[e][semi], 16 * cnt)

        for e, eng_deco in engines.items():
            eng_deco(lambda eng, e=e: run(e, eng))


D = 1024
NQ_TOK = 1024
EPS = 1e-6
WAVES = [(0, 4), (4, 8), (8, 12), (12, 16), (16, 20), (20, 22)]
SUBS = [(0, 342), (342, 342), (684, 340)]
D_FF = 2816
D_IN = 2320
MIX_STOP = 99
ARENA_WORDS = 12488


class Arena:
    def __init__(self, t, words):
        self.t = t
        self.words = words
        self.off = 0

    def reset(self):
        self.off = 0

    def alloc(self, shape, dt):
        n = 1
        for s_ in shape[1:]:
            n *= s_
        esz = 4 if dt == F32 else 2
        words = (n * esz + 3) // 4
        words = (words + 1) // 2 * 2
        a = self.t[:, self.off:self.off + words]
        self.off += words
        assert self.off <= self.words, ("arena overflow", self.off, self.words)
        if dt != F32:
            a = a.bitcast(dt)
        a = a[:, 0:n]
        if len(shape) > 2:
            names = "abcd"[:len(shape) - 1]
            pat = "p (" + " ".join(names) + ") -> p " + " ".join(names)
            a = a.rearrange(pat, **{nm: int(sz) for nm, sz in zip(names, shape[1:])})
        return a


def build(T, phases=("mix", "cross", "ffn")):
    nc = bass.Bass("TRN2", target_bir_lowering=False)
    MIX, CROSS, FFN = ("mix" in phases), ("cross" in phases), ("ffn" in phases)

    def din(name, shape):
        return nc.dram_tensor(name, list(shape), F32, kind="ExternalInput").ap()

    x = din("x", [T, D])
    gvec_in = din("gvec", [5, 128, 8])
    ident_in = din("ident", [128, 128])
    y = nc.dram_tensor("y", [T, D], F32, kind="ExternalOutput").ap()
    if FFN:
        convp_in = din("convp", [128, 44, 4])
        w_up = din("w_up", [D, 2 * D_FF])
        w_down = din("w_down", [D_FF, D])
        w_up_v = w_up.rearrange("(kc p) (gv j n) -> p kc gv j n", p=128, gv=2, n=128)
        w_down_v = w_down.rearrange("(j p) n -> p j n", p=128)
    if CROSS:
        mem = din("mem", [256, D])
        wq_v = din("w_q_c", [D, D]).rearrange("(kc p) n -> p kc n", p=128)
        wk_v = din("w_k_c", [D, D]).rearrange("(kc p) n -> p kc n", p=128)
        wv_v = din("w_v_c", [D, D]).rearrange("(kc p) n -> p kc n", p=128)
        wo_v = din("w_o_c", [D, D]).rearrange("(kc p) n -> p kc n", p=128)
    if MIX:
        w_in_v = din("w_in", [D, D_IN]).rearrange("(kc p) n -> p kc n", p=128)
        w_out_v = din("w_out", [D, D]).rearrange("(kc p) n -> p kc n", p=128)
        walpha_in = din("walpha", [32, 256])
        gain_in = din("gain", [128, 1])
        maskA_in = din("maskA", [128, 128])
        tri_in = din("tri", [2, 128, 128])
        bias_in = din("swabias", [128, 2, 2, 512])
        negm_in = din("negmask", [128, 2, 512])
        sinks_in = din("sinks_b", [128, 8])

    NQ = T // NQ_TOK
    with ExitStack() as es:
        def sb(name, shape, dt):
            return es.enter_context(nc.sbuf_tensor(name, shape, dt))

        hT = sb("hT", [128, 8, NQ_TOK], F32)
        ident = sb("identf", [128, 128], F32)
        ones_bf = sb("ones_bf", [128, 128], BF16)
        ones1 = sb("ones1", [128, 128], BF16)
        gvec = sb("gvec_sb", [128, 5, 8], F32)
        G_FIN, G_FFN, G_CROSS, G_MEM, G_MIX = range(5)
        if FFN:
            convp = sb("convp_sb", [128, 44, 4], F32)
            wup = sb("wup", [128, 3, 8, 2, 128], BF16)
            wdn = sb("wdn", [128, 4, 1024], BF16)
            halo = sb("halo", [128, 8, 2], BF16)
        if CROSS:
            wq = sb("wq", [128, 8, 1024], BF16)
            wo = sb("wo", [128, 8, 1024], BF16)
            KT = sb("KT", [128, 8, 256], BF16)
            Vm = sb("Vm", [128, 2, 1024], BF16)
        if MIX:
            w_in = sb("w_in_sb", [128, 8, 2192], BF16)
            skdup = sb("skdup", [128, 8, 256], BF16)
            w_out = sb("w_out_sb", [128, 8, 1024], BF16)
            identb = sb("identb", [128, 128], BF16)
            walpha = sb("walpha_sb", [128, 256], BF16)
            gain = sb("gain_sb", [128, 1], F32)
            maskA = sb("maskA_sb", [128, 128], BF16)
            triLE = sb("triLE", [128, 128], F32)
            triGT = sb("triGT", [128, 128], F32)
            biasm = sb("biasm", [128, 2, 2, 512], BF16)
            esink = sb("esink", [128, 8], F32)
            S = sb("S_state", [128, 2, 128], F32)
            Sbf = sb("S_bf", [128, 2, 128], BF16)
            skT = sb("skT", [128, 2, 640], BF16)
            vext = sb("vext", [128, 2, 2, 65], BF16)
        arena_t = sb("arena", [128, ARENA_WORDS], F32)
        A = Arena(arena_t, ARENA_WORDS)

        banks = [es.enter_context(nc.psum_tensor(f"bank{i}", [128, 512], F32)) for i in range(8)]
        banks_bf = [bk.bitcast(BF16) for bk in banks]
        sems = {e: es.enter_context(nc.semaphore(f"s_{e}")) for e in COMPUTE}
        dsems = {q: [es.enter_context(nc.semaphore(f"d_{q}{i}")) for i in range(N_DMA_SEMS)]
                 for q in ("sp", "pool")}

        P = Prog(nc)

        A.reset()
        A.off = 7168
        xtok = A.alloc([128, 4, 1024], F32)
        A.reset()
        f_sq = A.alloc([128, 2, 512], BF16)
        f_rstd = A.alloc([128, 512], F32)
        ytok = A.alloc([128, 2, 1024], F32)
        yT = A.alloc([128, 8, 512], F32)
        if FFN:
            A.reset()
            c_sq = A.alloc([128, 2, 512], BF16)
            c_rstd = A.alloc([128, 512], F32)
            hn3 = A.alloc([128, 8, NQ_TOK + 2], BF16)
            actT = A.alloc([128, 2, 4, NQ_TOK], BF16)
            tA = A.alloc([128, 2, 2, 344], F32)
            tB = A.alloc([128, 2, 2, 344], F32)
        if CROSS:
            A.reset()
            b_sq = A.alloc([128, 2, 512], BF16)
            b_rstd = A.alloc([128, 512], F32)
            hn2 = A.alloc([128, 8, 512], BF16)
            cq = A.alloc([128, 8, 512], BF16)
            pT = A.alloc([128, 2, 2, 512], BF16)
            rden = A.alloc([128, 2, 512], F32)
            onT = A.alloc([128, 8, 512], BF16)
            A.reset()
            k_sq = A.alloc([128, 2, 512], BF16)
            k_rstd = A.alloc([128, 512], F32)
            k_xtok = A.alloc([128, 2, 1024], F32)
            memT = A.alloc([128, 8, 256], F32)
            memn = A.alloc([128, 8, 256], BF16)
            wk = A.alloc([128, 8, 1024], BF16)
            wv = wk
        if MIX:
            A.reset()
            bias_f = A.alloc([128, 2, 2, 512], F32)
            negm = A.alloc([128, 2, 512], F32)
            sinks_b = A.alloc([128, 8], F32)
            A.reset()
            er = A.alloc([128, 512], F32)
            m_sq = er.bitcast(BF16).rearrange("p (a b) -> p a b", a=2)
            m_rstd = A.alloc([128, 512], F32)
            alow = m_rstd.bitcast(BF16)[:, 0:512]
            junk = m_rstd[:, 256:384]
            hnT = A.alloc([128, 8, 512], BF16)
            L = A.alloc([128, 1024], F32)
            Eend = L
            Ebc = A.alloc([128, 2, 512], F32)
            Einv = A.alloc([128, 2, 512], F32)
            qdec = A.alloc([128, 2, 512], BF16)
            kinv = A.alloc([128, 2, 512], BF16)
            sqT = A.alloc([128, 4, 512], BF16)
            kte = A.alloc([128, 256], BF16)
            vtok = A.alloc([128, 512], BF16)
            attTm = A.alloc([128, 512], BF16)
            pTm = A.alloc([128, 4, 512], BF16)
            ss = A.alloc([128, 4], F32)
            den2 = A.alloc([128, 8], F32)
            omix = A.alloc([128, 1024], BF16)
            omixT = A.alloc([128, 8, 512], BF16)

        P.dma("sp", ident[:], ident_in, writes=[("ident",)])
        for i in range(5):
            P.dma("sp", gvec[:, i, :], gvec_in[i], writes=[("gvec",)])
        P.op("dve", lambda e: e.memset(ones_bf[:], 1.0 / D), writes=[("ones",)])
        P.op("dve", lambda e: e.memset(ones1[:], 1.0), writes=[("ones1",)])
        if FFN:
            P.dma("sp", convp[:], convp_in, writes=[("convp",)])
            P.op("dve", lambda e: e.memset(halo[:], 0.0), writes=[("halo",)])
        if MIX:
            P.dma("pool", w_in[:, :, 0:512], w_in_v[:, :, 0:512], writes=[("w_in", 0)])
            P.dma("pool", w_in[:, :, 512:640], w_in_v[:, :, 2192:2320], writes=[("w_in", 1)])
            P.dma("pool", w_in[:, :, 640:2192], w_in_v[:, :, 512:2064], writes=[("w_in", 2)])
            for kvh in range(2):
                for d_ in range(2):
                    P.dma("pool", skdup[:, :, kvh * 128 + d_ * 64: kvh * 128 + d_ * 64 + 64],
                          w_in_v[:, :, 2064 + kvh * 64: 2064 + kvh * 64 + 64],
                          writes=[("skdup", kvh, d_)])
            P.dma("pool", w_out[:], w_out_v, writes=[("w_out",)])
            P.dma("pool", identb[:], ident_in, writes=[("identb",)])
            P.dma("pool", walpha[0:32, :], walpha_in, writes=[("walpha",)])
            P.dma("pool", maskA[:], maskA_in, writes=[("maskA",)])
            P.dma("sp", gain[:], gain_in, writes=[("gain",)])
            P.dma("sp", triLE[:], tri_in[0], writes=[("triLE",)])
            P.dma("sp", triGT[:], tri_in[1], writes=[("triGT",)])
            P.dma("sp", bias_f[:], bias_in, writes=[("@s.bias",)])
            P.dma("sp", negm[:], negm_in, writes=[("@s.negm",)])
            P.dma("sp", sinks_b[:], sinks_in, writes=[("@s.sinks",)])
            for kvh in range(2):
                P.op("dve", lambda e, kvh=kvh: e.tensor_tensor(
                    out=biasm[:, kvh, :, :], in0=bias_f[:, kvh, :, :], in1=negm[:, :, :],
                    op=ALU.add), reads=[("@s.bias",), ("@s.negm",)], writes=[("biasm",)])
            P.op("act", lambda e: e.activation(out=esink[:], in_=sinks_b[:], func=AF.Exp),
                 reads=[("@s.sinks",)], writes=[("esink",)])
            P.op("dve", lambda e: e.memset(S[:], 0.0), writes=[("S",)])
            P.op("dve", lambda e: e.memset(Sbf[:], 0.0), writes=[("Sbf",)])
            P.op("dve", lambda e: e.memset(skT[:], 0.0), writes=[("skT", 0), ("skT", 1)])
            P.op("dve", lambda e: e.memset(vext[:], 1.0), writes=[("vext", 0), ("vext", 1)])
            P.fence()

        W_IN_KEYS = [("w_in", 0), ("w_in", 1), ("w_in", 2)]

        def rms_stats(src_fn, src_keys, ncols, sq, sq_key, out_rstd, out_key):
            b = P.bank()
            for c in range(8):
                s = c % 2
                if c % 2 == 0:
                    P.op("act", lambda e, c=c, s=s: e.activation(
                        out=sq[:, s, :ncols], in_=src_fn(c), func=AF.Square),
                        reads=[src_keys[c]], writes=[(sq_key, s)])
                else:
                    P.op("dve", lambda e, c=c, s=s: e.tensor_tensor(
                        out=sq[:, s, :ncols], in0=src_fn(c), in1=src_fn(c), op=ALU.mult),
                        reads=[src_keys[c]], writes=[(sq_key, s)])
                P.op("pe", lambda e, c=c, s=s, b=b: e.matmul(
                    banks[b][:, :ncols], lhsT=ones_bf[:], rhs=sq[:, s, :ncols],
                    start=(c == 0), stop=(c == 7)),
                    reads=[(sq_key, s), ("ones",)], writes=[("ps", b)])
            P.op("act", lambda e, b=b: e.activation(
                out=out_rstd[:, :ncols], in_=banks[b][:, :ncols], func=AF.Ln, bias=float(EPS)),
                reads=[("ps", b)], writes=[out_key])
            P.op("act", lambda e: e.activation(
                out=out_rstd[:, :ncols], in_=out_rstd[:, :ncols], func=AF.Exp, scale=-0.5),
                reads=[out_key], writes=[out_key])

        def hslice(c, g):
            return hT[:, c, g * 512:(g + 1) * 512]

        def hkeys(g):
            return [("hT", c, g) for c in range(8)]

        def evac(i, out_ap, bank_ap, reads, writes, scale=None):
            if i % 2 == 0:
                if scale is None:
                    P.op("act", lambda e: e.copy(out=out_ap, in_=bank_ap), reads=reads,
                         writes=writes)
                else:
                    P.op("act", lambda e: e.mul(out_ap, bank_ap, float(scale)),
                         reads=reads, writes=writes)
            else:
                if scale is None:
                    P.op("dve", lambda e: e.tensor_copy(out=out_ap, in_=bank_ap),
                         reads=reads, writes=writes)
                else:
                    P.op("dve", lambda e: e.tensor_scalar_mul(out_ap, bank_ap, float(scale)),
                         reads=reads, writes=writes)

        def norm_apply(out_fn, out_key_fn, g, gi, rstd_t, rstd_key):
            for c in range(8):
                P.op("dve", lambda e, c=c: e.scalar_tensor_tensor(
                    out=out_fn(c), in0=hslice(c, g), scalar=gvec[:, gi, c:c + 1],
                    in1=rstd_t[:, :], op0=ALU.mult, op1=ALU.mult),
                    reads=[("hT", c, g), rstd_key, ("gvec",)], writes=[out_key_fn(c)])

        def proj_block(lhs_fn, lhs_keys, rhs_fn, rhs_key_fn, out_fn, b):
            for kc in range(8):
                P.op("pe", lambda e, kc=kc: e.matmul(
                    out_fn(), lhsT=lhs_fn(kc), rhs=rhs_fn(kc), start=(kc == 0), stop=(kc == 7)),
                    reads=list(lhs_keys) + [rhs_key_fn(kc)], writes=[("ps", b)])

        def resid_add(b, fo, g):
            P.op("dve", lambda e: e.tensor_tensor(
                out=hslice(fo, g), in0=hslice(fo, g), in1=banks[b][:, :], op=ALU.add),
                reads=[("ps", b), ("hT", fo, g)], writes=[("hT", fo, g)])

        def load_quarter(q):
            xk = "@x.xtok" if q == 0 else "@l.xtok"
            for g in range(NQ_TOK // 512):
                tok0 = q * NQ_TOK + g * 512
                for t in range(4):
                    P.dma("sp", xtok[:, t, :], x[tok0 + t * 128: tok0 + (t + 1) * 128, :],
                          writes=[(xk, t)])
                for c in range(8):
                    b = P.bank()
                    for t in range(4):
                        P.op("pe", lambda e, c=c, t=t, b=b: e.transpose(
                            banks[b][:, t * 128:(t + 1) * 128], xtok[:, t, c * 128:(c + 1) * 128],
                            ident[:]),
                            reads=[(xk, t), ("ident",)], writes=[("ps", b)])
                    evac(c, hslice(c, g), banks[b][:, :], [("ps", b)], [("hT", c, g)])

        def mix_tile_A(G, g, t, n, par, tsl, hn_keys, lhs_tok, ctx):
            bK = P.bank()
            proj_block(lhs_tok, W_IN_KEYS, lambda kc: w_in[:, kc, 256:640], hn_keys,
                       lambda: banks[bK][:, 0:384], bK)
            bV = P.bank()
            proj_block(lhs_tok, W_IN_KEYS, lambda kc: w_in[:, kc, 640:1152], hn_keys,
                       lambda: banks[bV][:, :], bV)
            bR = P.bank()
            proj_block(lhs_tok, W_IN_KEYS, lambda kc: w_in[:, kc, 1152:1664], hn_keys,
                       lambda: banks[bR][:, :], bR)
            P.op("dve", lambda e: e.tensor_tensor(
                out=kte[:, :], in0=banks[bK][:, 0:256], in1=Eend[:, t * 256:(t + 1) * 256],
                op=ALU.mult), reads=[("ps", bK), ("@m.L", t // 2)], writes=[("@m.kte",)])
            P.op("act", lambda e: e.copy(out=vtok[:, :], in_=banks[bV][:, :]),
                 reads=[("ps", bV)], writes=[("@m.v",)])
            P.op("dve", lambda e: e.tensor_copy(
                out=vext[:, par, :, 0:64],
                in_=banks[bK][:, 256:384].rearrange("p (a d) -> p a d", a=2)),
                reads=[("ps", bK)], writes=[("vext", par)])
            P.op("act", lambda e: e.activation(out=er[:, :], in_=banks[bR][:, :], func=AF.Exp,
                                               scale=-1.0),
                 reads=[("ps", bR)], writes=[("@m.sq", 0), ("@m.sq", 1)])
            P.op("dve", lambda e: e.tensor_scalar_add(er[:, :], er[:, :], 1.0),
                 reads=[("@m.sq", 0), ("@m.sq", 1)], writes=[("@m.sq", 0), ("@m.sq", 1)])
            P.op("dve", lambda e: e.reciprocal(out=er[:, :], in_=er[:, :]),
                 reads=[("@m.sq", 0), ("@m.sq", 1)], writes=[("@m.sq", 0), ("@m.sq", 1)])
            P.op("dve", lambda e: e.tensor_tensor(out=er[:, :], in0=banks[bR][:, :],
                                                  in1=er[:, :], op=ALU.mult),
                 reads=[("ps", bR), ("@m.sq", 0), ("@m.sq", 1)], writes=[("@m.sq", 0), ("@m.sq", 1)])
            kts = [1] if n == 0 else [0, 1]
            for kvh in range(2):
                for kt in kts:
                    koff = 128 + t * 128 if kt == 1 else t * 128
                    for pp in range(2):
                        bs = P.bank()
                        P.op("pe", lambda e, bs=bs, koff=koff, kvh=kvh, pp=pp: e.matmul(
                            banks[bs][:, 0:256],
                            lhsT=skT[pp * 64:(pp + 1) * 64, kvh, koff:koff + 128],
                            rhs=sqT[pp * 64:(pp + 1) * 64, 2 * kvh:2 * kvh + 2, tsl],
                            start=True, stop=False),
                            reads=[("skT", kvh), ("@m.sqT", 2 * kvh), ("@m.sqT", 2 * kvh + 1)],
                            writes=[("ps", bs)])
                        P.op("pe", lambda e, bs=bs, kvh=kvh, kt=kt, pp=pp: e.matmul(
                            banks[bs][:, 0:256], lhsT=identb[:, :],
                            rhs=biasm[:, kvh, kt, pp * 256:(pp + 1) * 256],
                            start=False, stop=True),
                            reads=[("identb",), ("biasm",)], writes=[("ps", bs)])
                        P.op("act", lambda e, bs=bs, kvh=kvh, kt=kt, pp=pp: e.activation(
                            out=pTm[:, kvh * 2 + kt, :].rearrange(
                                "p (bb pp q) -> p pp bb q", bb=2, pp=2)[:, pp],
                            in_=banks[bs][:, 0:256].rearrange("p (bb q) -> p bb q", bb=2),
                            func=AF.Exp),
                            reads=[("ps", bs)], writes=[("@m.pT", kvh * 2 + kt)])
            bA2 = [P.bank(), P.bank()]
            for h in range(4):
                hp, hb = h % 2, h // 2
                P.op("pe", lambda e, h=h, hp=hp, hb=hb: e.matmul(
                    banks[bA2[hp]][:, hb * 128:(hb + 1) * 128],
                    lhsT=kinv[hp * 64:(hp + 1) * 64, hb, tsl],
                    rhs=qdec[hp * 64:(hp + 1) * 64, hb, tsl], start=True, stop=True),
                    reads=[("@m.kinv", hb), ("@m.qdec", hb)], writes=[("ps", bA2[hp])])
            for hp in range(2):
                P.op("dve", lambda e, hp=hp: e.tensor_tensor(
                    out=attTm[:, :].rearrange("p (hb hp i) -> p hp hb i", hb=2, hp=2)[:, hp],
                    in0=banks[bA2[hp]][:, 0:256].rearrange("p (hb i) -> p hb i", hb=2),
                    in1=maskA[:, :].unsqueeze(1).to_broadcast([128, 2, 128]), op=ALU.mult),
                    reads=[("ps", bA2[hp]), ("maskA",)], writes=[("@m.att",)])
            ctx["bR"] = bR
            ctx["bA2"] = bA2

        def mix_tile_B(G, g, t, n, par, tsl, ctx):
            bR = ctx["bR"]
            bA2 = ctx["bA2"]
            bKV = P.bank()
            for h in range(4):
                hp, hb = h % 2, h // 2
                P.op("pe", lambda e, h=h, hp=hp, hb=hb: e.matmul(
                    banks[bKV][hp * 64:(hp + 1) * 64, hb * 128:(hb + 1) * 128],
                    lhsT=kte[:, h * 64:(h + 1) * 64], rhs=vtok[:, h * 128:(h + 1) * 128],
                    start=True, stop=True),
                    reads=[("@m.kte",), ("@m.v",)], writes=[("ps", bKV)])
            bO2 = [P.bank(), P.bank()]
            for h in range(4):
                hp, hb = h % 2, h // 2
                P.op("pe", lambda e, h=h, hp=hp, hb=hb: e.matmul(
                    banks[bO2[hp]][:, hb * 128:(hb + 1) * 128],
                    lhsT=attTm[:, h * 128:(h + 1) * 128],
                    rhs=vtok[:, h * 128:(h + 1) * 128], start=True, stop=False),
                    reads=[("@m.att",), ("@m.v",)], writes=[("ps", bO2[hp])])
                P.op("pe", lambda e, h=h, hp=hp, hb=hb: e.matmul(
                    banks[bO2[hp]][:, hb * 128:(hb + 1) * 128],
                    lhsT=qdec[hp * 64:(hp + 1) * 64, hb, tsl],
                    rhs=Sbf[hp * 64:(hp + 1) * 64, hb, :], start=False, stop=True),
                    reads=[("@m.qdec", hb), ("Sbf",)], writes=[("ps", bO2[hp])])
            for hb in range(2):
                P.op("dve", lambda e, hb=hb: e.scalar_tensor_tensor(
                    out=S[:, hb, :], in0=S[:, hb, :],
                    scalar=Ebc[:, hb, t * 128 + 127:t * 128 + 128],
                    in1=banks[bKV][:, hb * 128:(hb + 1) * 128], op0=ALU.mult, op1=ALU.add),
                    reads=[("S",), ("@m.Ebc", hb), ("ps", bKV)], writes=[("S",)])
            P.op("act", lambda e: e.copy(out=Sbf[:], in_=S[:]), reads=[("S",)],
                 writes=[("Sbf",)])
            P.op("dve", lambda e: e.memset(ss[:, :], 0.0), writes=[("@m.ss",)])
            for h in range(4):
                P.op("act", lambda e, h=h: e.activation(
                    out=junk[:, :], in_=banks[bO2[h % 2]][:, (h // 2) * 128:(h // 2 + 1) * 128],
                    func=AF.Square, accum_out=ss[:, h:h + 1]),
                    reads=[("ps", bO2[h % 2]), ("@m.ss",)], writes=[("@m.ss",), ("@m.rstd",)])
            P.op("act", lambda e: e.activation(out=ss[:, :], in_=ss[:, :], func=AF.Ln,
                                               bias=float(EPS), scale=1.0 / 128.0),
                 reads=[("@m.ss",)], writes=[("@m.ss",)])
            P.op("act", lambda e: e.activation(out=ss[:, :], in_=ss[:, :], func=AF.Exp,
                                               scale=-0.5),
                 reads=[("@m.ss",)], writes=[("@m.ss",)])
            P.op("dve", lambda e: e.tensor_tensor(
                out=er[:, :].rearrange("p (h v) -> p h v", h=4),
                in0=er[:, :].rearrange("p (h v) -> p h v", h=4),
                in1=ss[:, :].unsqueeze(2).to_broadcast([128, 4, 128]), op=ALU.mult),
                reads=[("@m.sq", 0), ("@m.sq", 1), ("@m.ss",)], writes=[("@m.sq", 0), ("@m.sq", 1)])
            for hp in range(2):
                P.op("dve", lambda e, hp=hp: e.tensor_tensor(
                    out=omix[:, 0:512].rearrange("p (hb hp v) -> p hp hb v", hb=2, hp=2)[:, hp],
                    in0=banks[bO2[hp]][:, 0:256].rearrange("p (hb v) -> p hb v", hb=2),
                    in1=er[:, :].rearrange("p (hb hp v) -> p hp hb v", hb=2, hp=2)[:, hp],
                    op=ALU.mult),
                    reads=[("ps", bO2[hp]), ("@m.sq", 0), ("@m.sq", 1)], writes=[("@m.omix", 0)])
            kts = [1] if n == 0 else [0, 1]
            for kvh in range(2):
                bo = P.bank()
                for j in range(4):
                    for idx, kt in enumerate(kts):
                        pk = par if kt == 1 else 1 - par
                        P.op("pe", lambda e, bo=bo, j=j, kt=kt, pk=pk, idx=idx, kvh=kvh: e.matmul(
                            banks[bo][:, j * 65:(j + 1) * 65],
                            lhsT=pTm[:, kvh * 2 + kt, j * 128:(j + 1) * 128],
                            rhs=vext[:, pk, kvh, 0:65], start=(idx == 0),
                            stop=(idx == len(kts) - 1)),
                            reads=[("@m.pT", kvh * 2 + kt), ("vext", pk)], writes=[("ps", bo)])
                ov = banks[bo][:, 0:260].rearrange("p (j e) -> p j e", e=65)
                P.op("dve", lambda e, ov=ov, kvh=kvh: e.tensor_tensor(
                    out=den2[:, kvh * 4:(kvh + 1) * 4], in0=ov[:, :, 64],
                    in1=esink[:, kvh * 4:(kvh + 1) * 4], op=ALU.add),
                    reads=[("ps", bo), ("esink",)], writes=[("@m.den", kvh)])
                P.op("dve", lambda e, kvh=kvh: e.reciprocal(
                    out=den2[:, kvh * 4:(kvh + 1) * 4], in_=den2[:, kvh * 4:(kvh + 1) * 4]),
                    reads=[("@m.den", kvh)], writes=[("@m.den", kvh)])
                P.op("dve", lambda e, ov=ov, kvh=kvh: e.tensor_tensor(
                    out=omix[:, 512 + kvh * 256: 512 + (kvh + 1) * 256].rearrange(
                        "p (j d) -> p j d", d=64),
                    in0=ov[:, :, 0:64],
                    in1=den2[:, kvh * 4:(kvh + 1) * 4].unsqueeze(2).to_broadcast([128, 4, 64]),
                    op=ALU.mult),
                    reads=[("ps", bo), ("@m.den", kvh)], writes=[("@m.omix", 1 + kvh)])

        def mix_tile_C(G, g, t, n, par, tsl, ctx):
            bT2 = [P.bank(), P.bank()]
            for c in range(8):
                P.op("pe", lambda e, c=c: e.matmul(
                    banks[bT2[c // 4]][:, (c % 4) * 128:(c % 4 + 1) * 128],
                    lhsT=omix[:, c * 128:(c + 1) * 128], rhs=identb[:, :], start=True, stop=True),
                    reads=[("@m.omix", 0 if c < 4 else (1 if c < 6 else 2)), ("identb",)],
                    writes=[("ps", bT2[c // 4])])
            P.op("act", lambda e: e.activation(
                out=omixT[:, 0:4, tsl],
                in_=banks[bT2[0]][:, :].rearrange("p (c k) -> p c k", c=4),
                func=AF.Identity, scale=gain[:, 0:1]),
                reads=[("ps", bT2[0]), ("gain",)], writes=[("@m.omixT", 0)])
            P.op("dve", lambda e: e.tensor_copy(
                out=omixT[:, 4:8, tsl],
                in_=banks[bT2[1]][:, :].rearrange("p (c k) -> p c k", c=4)),
                reads=[("ps", bT2[1])], writes=[("@m.omixT", 1)])

        def mix_tile(G, g, t, st, ctx):
            n = G * 4 + t
            par = n % 2
            tsl = slice(t * 128, (t + 1) * 128)
            hn_keys = lambda kc: ("@m.hn", kc)
            lhs_tok = lambda kc: hnT[:, kc, tsl]
            if st == "A":
                mix_tile_A(G, g, t, n, par, tsl, hn_keys, lhs_tok, ctx)
            elif st == "B":
                mix_tile_B(G, g, t, n, par, tsl, ctx)
            else:
                mix_tile_C(G, g, t, n, par, tsl, ctx)

        def mix_group(q, g):
            G = q * 2 + g
            rms_stats(lambda c: hslice(c, g), hkeys(g), 512, m_sq, "@m.sq", m_rstd, ("@m.rstd",))
            norm_apply(lambda c: hnT[:, c, :], lambda c: ("@m.hn", c), g, G_MIX, m_rstd,
                       ("@m.rstd",))
            hn_keys = lambda kc: ("@m.hn", kc)
            rhs_hn = lambda kc: hnT[:, kc, :]
            P.op("dve", lambda e: e.memset(alow[0:32, :], 1.0), writes=[("@m.rstd",)])
            b = P.bank()
            proj_block(lambda kc: w_in[:, kc, 1664:1680], W_IN_KEYS, rhs_hn, hn_keys,
                       lambda b=b: banks[b][0:16, :], b)
            P.op("act", lambda e, b=b: e.copy(out=alow[0:16, :], in_=banks[b][0:16, :]),
                 reads=[("ps", b)], writes=[("@m.rstd",)])
            for b4 in range(4):
                b = P.bank()
                proj_block(lambda kc, b4=b4: w_in[:, kc, 1680 + b4 * 128:1680 + (b4 + 1) * 128],
                           W_IN_KEYS, rhs_hn, hn_keys, lambda b=b: banks[b][:, :], b)
                evac(b4, sqT[:, b4, :], banks[b][:, :], [("ps", b)], [("@m.sqT", b4)],
                     scale=0.125)
            for kvh in range(2):
                b = P.bank()
                proj_block(lambda kc, kvh=kvh: skdup[:, kc, kvh * 128:(kvh + 1) * 128],
                           [("skdup", kvh, 0), ("skdup", kvh, 1)], rhs_hn, hn_keys,
                           lambda b=b: banks[b][:, :], b)
                evac(kvh, skT[:, kvh, 128:640], banks[b][:, :], [("ps", b)], [("skT", kvh)])
            if MIX_STOP <= 1:
                return
            zb = [P.bank(), P.bank()]
            for t in range(4):
                P.op("pe", lambda e, t=t: e.matmul(
                    banks[zb[t // 2]][:, (t % 2) * 256:(t % 2 + 1) * 256],
                    lhsT=alow[0:32, t * 128:(t + 1) * 128], rhs=walpha[0:32, :],
                    start=True, stop=True),
                    reads=[("@m.rstd",), ("walpha",)], writes=[("ps", zb[t // 2])])
            for i in range(2):
                P.op("act", lambda e, i=i: e.activation(
                    out=L[:, i * 512:(i + 1) * 512], in_=banks[zb[i]][:, :], func=AF.Exp,
                    scale=-1.0), reads=[("ps", zb[i])], writes=[("@m.L", i)])
                P.op("act", lambda e, i=i: e.activation(
                    out=L[:, i * 512:(i + 1) * 512], in_=L[:, i * 512:(i + 1) * 512],
                    func=AF.Ln, bias=1.0), reads=[("@m.L", i)], writes=[("@m.L", i)])
            cb = [P.bank(), P.bank()]
            for hb in range(2):
                for t in range(4):
                    P.op("pe", lambda e, hb=hb, t=t: e.matmul(
                        banks[cb[hb]][:, t * 128:(t + 1) * 128],
                        lhsT=L[:, t * 256 + hb * 128: t * 256 + (hb + 1) * 128],
                        rhs=triLE[:, :], start=True, stop=True),
                        reads=[("@m.L", t // 2), ("triLE",)], writes=[("ps", cb[hb])])
            te = [P.bank(), P.bank()]
            for t in range(4):
                P.op("pe", lambda e, t=t: e.matmul(
                    banks[te[t // 2]][:, (t % 2) * 256:(t % 2 + 1) * 256],
                    lhsT=triGT[:, :], rhs=L[:, t * 256:(t + 1) * 256], start=True, stop=True),
                    reads=[("@m.L", t // 2), ("triGT",)], writes=[("ps", te[t // 2])])
            for hb in range(2):
                P.op("act", lambda e, hb=hb: e.activation(
                    out=Ebc[:, hb, :], in_=banks[cb[hb]][:, :], func=AF.Exp),
                    reads=[("ps", cb[hb])], writes=[("@m.Ebc", hb)])
                P.op("act", lambda e, hb=hb: e.activation(
                    out=Einv[:, hb, :], in_=banks[cb[hb]][:, :], func=AF.Exp, scale=-1.0),
                    reads=[("ps", cb[hb])], writes=[("@m.Einv", hb)])
            for i in range(2):
                P.op("act", lambda e, i=i: e.activation(
                    out=Eend[:, i * 512:(i + 1) * 512], in_=banks[te[i]][:, :], func=AF.Exp),
                    reads=[("ps", te[i])], writes=[("@m.L", i)])
            for hb in range(2):
                b = P.bank()
                proj_block(lambda kc, hb=hb: w_in[:, kc, hb * 128:(hb + 1) * 128], W_IN_KEYS,
                           rhs_hn, hn_keys, lambda b=b: banks[b][:, :], b)
                P.op("dve", lambda e, b=b, hb=hb: e.scalar_tensor_tensor(
                    out=qdec[:, hb, :], in0=banks[b][:, :], scalar=0.125, in1=Ebc[:, hb, :],
                    op0=ALU.mult, op1=ALU.mult),
                    reads=[("ps", b), ("@m.Ebc", hb)], writes=[("@m.qdec", hb)])
            for hb in range(2):
                b = P.bank()
                proj_block(lambda kc, hb=hb: w_in[:, kc, 256 + hb * 128:256 + (hb + 1) * 128],
                           W_IN_KEYS, rhs_hn, hn_keys, lambda b=b: banks[b][:, :], b)
                P.op("dve", lambda e, b=b, hb=hb: e.tensor_tensor(
                    out=kinv[:, hb, :], in0=banks[b][:, :], in1=Einv[:, hb, :], op=ALU.mult),
                    reads=[("ps", b), ("@m.Einv", hb)], writes=[("@m.kinv", hb)])
            if MIX_STOP <= 2:
                return
            ctxs = [dict() for _ in range(4)]
            for t in range(4):
                mix_tile(G, g, t, "A", ctxs[t])
                if t > 0:
                    mix_tile(G, g, t - 1, "C", ctxs[t - 1])
                mix_tile(G, g, t, "B", ctxs[t])
            mix_tile(G, g, 3, "C", ctxs[3])
            if MIX_STOP <= 7:
                return
            for fo in range(8):
                b = P.bank()
                proj_block(lambda kc, fo=fo: w_out[:, kc, fo * 128:(fo + 1) * 128], [("w_out",)],
                           lambda kc: omixT[:, kc, :], lambda kc: ("@m.omixT", kc // 4),
                           lambda b=b: banks[b][:, :], b)
                resid_add(b, fo, g)
            P.op("dve", lambda e: e.tensor_copy(out=skT[:, :, 0:128], in_=skT[:, :, 512:640]),
                 reads=[("skT", 0), ("skT", 1)], writes=[("skT", 0), ("skT", 1)])

        def mem_kv():
            for t in range(2):
                P.dma("sp", k_xtok[:, t, :], mem[t * 128:(t + 1) * 128, :],
                      writes=[("@k.xtok", t)])
            P.dma("pool", wk[:], wk_v, writes=[("@k.wk",)])
            for c in range(8):
                b = P.bank()
                for t in range(2):
                    P.op("pe", lambda e, c=c, t=t, b=b: e.transpose(
                        banks[b][:, t * 128:(t + 1) * 128], k_xtok[:, t, c * 128:(c + 1) * 128],
                        ident[:]),
                        reads=[("@k.xtok", t), ("ident",)], writes=[("ps", b)])
                evac(c, memT[:, c, :], banks[b][:, 0:256], [("ps", b)], [("@k.memT", c)])
            rms_stats(lambda c: memT[:, c, :], [("@k.memT", c) for c in range(8)], 256,
                      k_sq, "@k.sq", k_rstd, ("@k.rstd",))
            for c in range(8):
                P.op("dve", lambda e, c=c: e.scalar_tensor_tensor(
                    out=memn[:, c, :], in0=memT[:, c, :], scalar=gvec[:, G_MEM, c:c + 1],
                    in1=k_rstd[:, 0:256], op0=ALU.mult, op1=ALU.mult),
                    reads=[("@k.memT", c), ("@k.rstd",), ("gvec",)], writes=[("@k.memn", c)])
            for dc in range(8):
                b = P.bank()
                proj_block(lambda kc, dc=dc: wk[:, kc, dc * 128:(dc + 1) * 128], [("@k.wk",)],
                           lambda kc: memn[:, kc, :], lambda kc: ("@k.memn", kc),
                           lambda b=b: banks[b][:, 0:256], b)
                evac(dc, KT[:, dc, :], banks[b][:, 0:256], [("ps", b)], [("KT",)])
            P.dma("pool", wv[:], wv_v, writes=[("@k.wk",)])
            for mt in range(2):
                for half in range(2):
                    b = P.bank()
                    proj_block(lambda kc, mt=mt: memn[:, kc, mt * 128:(mt + 1) * 128],
                               [("@k.wk",)],
                               lambda kc, half=half: wv[:, kc, half * 512:(half + 1) * 512],
                               lambda kc: ("@k.memn", kc), lambda b=b: banks[b][:, :], b)
                    evac(half, Vm[:, mt, half * 512:(half + 1) * 512], banks[b][:, :],
                         [("ps", b)], [("Vm",)])

        def cross_head(h, hp, part):
            if part == 0:
                cross_head_scores(h, hp)
            else:
                cross_head_out(h, hp)

        def cross_head_scores(h, hp):
            for mt in range(2):
                b = P.bank()
                for d2 in range(2):
                    dc = 2 * h + d2
                    P.op("pe", lambda e, b=b, dc=dc, d2=d2, mt=mt: e.matmul(
                        banks[b][:, :], lhsT=KT[:, dc, mt * 128:(mt + 1) * 128], rhs=cq[:, dc, :],
                        start=(d2 == 0), stop=(d2 == 1)),
                        reads=[("KT",), ("@b.cq", dc)], writes=[("ps", b)])
                P.op("act", lambda e, b=b, mt=mt: e.activation(
                    out=pT[:, hp, mt, :], in_=banks[b][:, :], func=AF.Exp),
                    reads=[("ps", b)], writes=[("@b.pT", hp, mt)])

        def cross_head_out(h, hp):
            b = P.bank()
            for mt in range(2):
                P.op("pe", lambda e, b=b, mt=mt: e.matmul(
                    banks[b][:, :], lhsT=ones1[:], rhs=pT[:, hp, mt, :],
                    start=(mt == 0), stop=(mt == 1)),
                    reads=[("ones1",), ("@b.pT", hp, mt)], writes=[("ps", b)])
            P.op("dve", lambda e, b=b: e.reciprocal(out=rden[:, hp, :], in_=banks[b][:, :]),
                 reads=[("ps", b)], writes=[("@b.rden", hp)])
            for d2 in range(2):
                dc = 2 * h + d2
                b = P.bank()
                for mt in range(2):
                    P.op("pe", lambda e, b=b, dc=dc, mt=mt: e.matmul(
                        banks[b][:, :], lhsT=Vm[:, mt, dc * 128:(dc + 1) * 128],
                        rhs=pT[:, hp, mt, :], start=(mt == 0), stop=(mt == 1)),
                        reads=[("Vm",), ("@b.pT", hp, mt)], writes=[("ps", b)])
                P.op("dve", lambda e, b=b, dc=dc: e.tensor_tensor(
                    out=onT[:, dc, :], in0=banks[b][:, :], in1=rden[:, hp, :], op=ALU.mult),
                    reads=[("ps", b), ("@b.rden", hp)], writes=[("@b.onT", dc)])

        def cross(q):
            for g in range(2):
                rms_stats(lambda c, g=g: hslice(c, g), hkeys(g), 512, b_sq, "@b.sq", b_rstd,
                          ("@b.rstd",))
                norm_apply(lambda c: hn2[:, c, :], lambda c: ("@b.hn2", c), g, G_CROSS, b_rstd,
                           ("@b.rstd",))
                for dc in range(8):
                    b = P.bank()
                    proj_block(lambda kc, dc=dc: wq[:, kc, dc * 128:(dc + 1) * 128], [("wq",)],
                               lambda kc: hn2[:, kc, :], lambda kc: ("@b.hn2", kc),
                               lambda b=b: banks[b][:, :], b)
                    evac(dc, cq[:, dc, :], banks[b][:, :], [("ps", b)], [("@b.cq", dc)],
                         scale=1.0 / 16.0)
                cross_head(0, 0, 0)
                for h in range(4):
                    if h + 1 < 4:
                        cross_head(h + 1, (h + 1) % 2, 0)
                    cross_head(h, h % 2, 1)
                for fo in range(8):
                    b = P.bank()
                    proj_block(lambda kc, fo=fo: wo[:, kc, fo * 128:(fo + 1) * 128], [("wo",)],
                               lambda kc: onT[:, kc, :], lambda kc: ("@b.onT", kc),
                               lambda b=b: banks[b][:, :], b)
                    resid_add(b, fo, g)

        def ffn_load_wup(j):
            slot = j % 3
            for gv in range(2):
                P.dma("pool", wup[:, slot, :, gv, :], w_up_v[:, :, gv, j, :],
                      writes=[("wup", slot, gv)])

        def ffn_load_wdn(w):
            a, bnd = WAVES[w]
            P.dma("pool", wdn[:, 0:bnd - a, :], w_down_v[:, a:bnd, :], writes=[("wdn",)])

        def ffn_prefetch(q):
            ffn_load_wup(0)
            ffn_load_wup(1)

        ffn_cnt = [0]

        def ffn_sub(j, jl, wbuf, slot, e0, n, ts):
            bks = []
            for gv in range(2):
                b = P.bank()
                bks.append(b)
                for kc in range(8):
                    P.op("pe", lambda e, b=b, kc=kc, gv=gv: e.matmul(
                        banks[b][:, :n + 2], lhsT=wup[:, slot, kc, gv, :],
                        rhs=hn3[:, kc, e0:e0 + n + 2], start=(kc == 0), stop=(kc == 7)),
                        reads=[("wup", slot, gv), ("@c.hn3", kc)], writes=[("ps", b)])
            for gv in range(2):
                jj = j + 22 * gv
                b = bks[gv]
                P.op("act", lambda e, b=b, jj=jj, gv=gv: e.activation(
                    out=tA[:, ts, gv, :n], in_=banks[b][:, 2:n + 2], func=AF.Identity,
                    bias=convp[:, jj, 3:4], scale=convp[:, jj, 2:3]),
                    reads=[("ps", b), ("convp",)], writes=[("@c.tA", ts, gv)])
                P.op("dve", lambda e, b=b, jj=jj, gv=gv: e.scalar_tensor_tensor(
                    out=tB[:, ts, gv, :n], in0=banks[b][:, 1:n + 1], scalar=convp[:, jj, 1:2],
                    in1=tA[:, ts, gv, :n], op0=ALU.mult, op1=ALU.add),
                    reads=[("ps", b), ("@c.tA", ts, gv), ("convp",)], writes=[("@c.tB", ts, gv)])
                P.op("dve", lambda e, b=b, jj=jj, gv=gv: e.scalar_tensor_tensor(
                    out=tA[:, ts, gv, :n], in0=banks[b][:, 0:n], scalar=convp[:, jj, 0:1],
                    in1=tB[:, ts, gv, :n], op0=ALU.mult, op1=ALU.add),
                    reads=[("ps", b), ("@c.tB", ts, gv), ("convp",)], writes=[("@c.tA", ts, gv)])
            P.op("act", lambda e: e.activation(
                out=tB[:, ts, 0, :n], in_=tA[:, ts, 0, :n], func=AF.Silu),
                reads=[("@c.tA", ts, 0)], writes=[("@c.tB", ts, 0)])
            P.op("pool", lambda e: e.tensor_tensor(
                out=actT[:, wbuf, jl, e0:e0 + n], in0=tB[:, ts, 0, :n], in1=tA[:, ts, 1, :n],
                op=ALU.mult),
                reads=[("@c.tB", ts, 0), ("@c.tA", ts, 1)], writes=[("@c.actT", wbuf, jl)])

        def ffn_pair(j, jl, wbuf):
            for (e0, n) in SUBS:
                ts = ffn_cnt[0] % 2
                ffn_cnt[0] += 1
                ffn_sub(j, jl, wbuf, j % 3, e0, n, ts)

        def ffn_down(w):
            a, bnd = WAVES[w]
            wbuf = w % 2
            cnt = bnd - a
            for fo in range(8):
                for s2 in range(2):
                    b = P.bank()
                    for jl in range(cnt):
                        P.op("pe", lambda e, b=b, jl=jl, fo=fo, s2=s2: e.matmul(
                            banks[b][:, :], lhsT=wdn[:, jl, fo * 128:(fo + 1) * 128],
                            rhs=actT[:, wbuf, jl, s2 * 512:(s2 + 1) * 512],
                            start=(jl == 0), stop=(jl == cnt - 1)),
                            reads=[("wdn",), ("@c.actT", wbuf, jl)], writes=[("ps", b)])
                    resid_add(b, fo, s2)

        def ffn(q):
            hn3_keys = [("@c.hn3", c) for c in range(8)]
            P.op("dve", lambda e: e.tensor_copy(out=hn3[:, :, 0:2], in_=halo[:, :, :]),
                 reads=[("halo",)], writes=hn3_keys)
            for g in range(2):
                rms_stats(lambda c, g=g: hslice(c, g), hkeys(g), 512, c_sq, "@c.sq", c_rstd,
                          ("@c.rstd",))
                norm_apply(lambda c, g=g: hn3[:, c, 2 + g * 512: 2 + (g + 1) * 512],
                           lambda c: ("@c.hn3", c), g, G_FFN, c_rstd, ("@c.rstd",))
            P.op("dve", lambda e: e.tensor_copy(out=halo[:, :, :],
                                                in_=hn3[:, :, NQ_TOK:NQ_TOK + 2]),
                 reads=hn3_keys, writes=[("halo",)])
            ffn_load_wdn(0)
            for w, (a, bnd) in enumerate(WAVES):
                for j in range(a, bnd):
                    if j + 2 < 22:
                        ffn_load_wup(j + 2)
                    ffn_pair(j, j - a, w % 2)
                    if j == a + 1 and w > 0:
                        ffn_down(w - 1)
                        ffn_load_wdn(w)
            ffn_down(len(WAVES) - 1)

        def final(q):
            for g in range(NQ_TOK // 512):
                tok0 = q * NQ_TOK + g * 512
                rms_stats(lambda c, g=g: hslice(c, g), hkeys(g), 512, f_sq, "@f.sq", f_rstd,
                          ("@f.rstd",))
                norm_apply(lambda c: yT[:, c, :], lambda c: ("@f.yT", c), g, G_FIN, f_rstd,
                           ("@f.rstd",))
                for t in range(4):
                    s = t % 2
                    for half in range(2):
                        b = P.bank()
                        for cc in range(4):
                            c = half * 4 + cc
                            P.op("pe", lambda e, c=c, cc=cc, t=t, b=b: e.transpose(
                                banks[b][:, cc * 128:(cc + 1) * 128],
                                yT[:, c, t * 128:(t + 1) * 128], ident[:]),
                                reads=[("@f.yT", c), ("ident",)], writes=[("ps", b)])
                        evac(half, ytok[:, s, half * 512:(half + 1) * 512], banks[b][:, :],
                             [("ps", b)], [("@f.ytok", s, half)])
                    P.dma("sp", y[tok0 + t * 128: tok0 + (t + 1) * 128, :], ytok[:, s, :],
                          reads=[("@f.ytok", s, 0), ("@f.ytok", s, 1)])

        if CROSS:
            mem_kv()
            P.fence()
            P.dma("pool", wq[:], wq_v, writes=[("wq",)])
            P.dma("pool", wo[:], wo_v, writes=[("wo",)])
        for q in range(NQ):
            if FFN:
                ffn_prefetch(q)
            load_quarter(q)
            P.fence()
            if MIX:
                for g in range(2):
                    mix_group(q, g)
                P.fence()
            if CROSS:
                cross(q)
                P.fence()
            if FFN:
                ffn(q)
                P.fence()
            final(q)
            P.fence()

        with nc.Block() as block:
            P.emit(sems, dsems, {"sp": block.sync, "pe": block.tensor, "act": block.scalar,
                                 "dve": block.vector, "pool": block.gpsimd})
    return nc


def _fm(v):
    return np.ascontiguousarray(np.asarray(v, np.float32).reshape(8, 128).T)


def _t5_bucket(dist):
    n = np.maximum(dist, 0)
    max_exact = 16
    nf = np.maximum(n, 1).astype(np.float32)
    large = max_exact + (np.log(nf / max_exact) / math.log(128 / max_exact)
                         * (32 - max_exact)).astype(np.int32)
    large = np.minimum(large, 31)
    return np.where(n < max_exact, n, large)


def host_consts(inputs, phases):
    c = {}
    c["ident"] = np.eye(128, dtype=np.float32)
    z8 = np.zeros(1024, np.float32)
    gv = [inputs["norm_final"], inputs["norm_ffn"][0], inputs["norm_cross"][0],
          inputs["norm_mem"][0], inputs["norm_mix"][0]]
    c["gvec"] = np.ascontiguousarray(np.stack([_fm(v) for v in gv], axis=0))
    if "cross" in phases:
        for k in ("w_q_c", "w_k_c", "w_v_c", "w_o_c"):
            c[k] = np.ascontiguousarray(inputs[k][0], dtype=np.float32)
    if "ffn" in phases:
        cw = np.asarray(inputs["conv_w"][0], np.float32)
        cb = np.asarray(inputs["conv_b"][0], np.float32)
        cp = np.concatenate([cw, cb[None]], axis=0)
        c["convp"] = np.ascontiguousarray(cp.reshape(4, 44, 128).transpose(2, 1, 0))
        c["w_up"] = np.ascontiguousarray(inputs["w_up"][0], dtype=np.float32)
        c["w_down"] = np.ascontiguousarray(inputs["w_down"][0], dtype=np.float32)
    if "mix" in phases:
        c["w_in"] = np.ascontiguousarray(inputs["w_in"][0], dtype=np.float32)
        c["w_out"] = np.ascontiguousarray(inputs["w_out"][0], dtype=np.float32)
        wa = np.zeros((32, 256), np.float32)
        wa[0:16] = np.asarray(inputs["w_alpha"][0], np.float32)
        wa[16] = np.asarray(inputs["b_alpha"][0], np.float32)
        c["walpha"] = wa
        c["gain"] = np.ascontiguousarray(
            np.asarray(inputs["gla_gain"][0], np.float32).reshape(128, 1))
        jj, ii = np.meshgrid(np.arange(128), np.arange(128), indexing="ij")
        causal = (jj <= ii).astype(np.float32)
        c["maskA"] = np.ascontiguousarray(causal)
        tri = np.zeros((2, 128, 128), np.float32)
        tri[0] = np.where(jj <= ii, -1.0 / 16.0, 0.0)
        tri[1] = np.where(jj > ii, -1.0 / 16.0, 0.0)
        c["tri"] = tri
        rel = np.asarray(inputs["rel_bias"], np.float32)
        kk, qq = np.meshgrid(np.arange(128), np.arange(128), indexing="ij")
        bias = np.zeros((128, 2, 2, 2, 2, 128), np.float32)
        negm = np.zeros((128, 2, 4, 128), np.float32)
        for kt in range(2):
            dist = (qq + 128 - kk) if kt == 0 else (qq - kk)
            ok = (dist >= 0) & (dist < 128)
            bk = _t5_bucket(dist)
            for kvh in range(2):
                for g in range(4):
                    bias[:, kvh, kt, g % 2, g // 2, :] = rel[bk, kvh * 4 + g]
            negm[:, kt, :, :] = np.where(ok, 0.0, -30000.0)[:, None, :]
        c["swabias"] = np.ascontiguousarray(bias.reshape(128, 2, 2, 512))
        c["negmask"] = np.ascontiguousarray(negm.reshape(128, 2, 512))
        c["sinks_b"] = np.ascontiguousarray(
            np.broadcast_to(np.asarray(inputs["sinks"][0], np.float32)[None, :], (128, 8)))
    return c


_NC_CACHE = {}
PHASES = ("mix", "cross", "ffn")
DEBUG = False
LAST_RES = None


def kernel(**inputs):
    x = np.asarray(inputs["x"], dtype=np.float32)
    B, T, _ = x.shape
    phases = tuple(PHASES)
    if (T, phases) not in _NC_CACHE:
        _NC_CACHE[(T, phases)] = build(T, phases)
    nc = _NC_CACHE[(T, phases)]
    c = host_consts({k: np.asarray(v) for k, v in inputs.items()}, phases)
    in_maps = []
    for b in range(B):
        m = dict(c)
        m["x"] = np.ascontiguousarray(x[b])
        if "cross" in phases:
            m["mem"] = np.ascontiguousarray(np.asarray(inputs["mem"], np.float32)[b])
        in_maps.append(m)
    res = run_bass_kernel_spmd(nc, in_maps, core_ids=list(range(B)))
    global LAST_RES
    LAST_RES = res.results
    return np.stack([np.asarray(r["y"], dtype=np.float32) for r in res.results], axis=0)
```
